# Optimizing a Trainium2 kernel written in Bass

```python
import math
import jax
import jax.numpy as jnp
from jax import lax
import numpy as np

D_MODEL = 1024
BATCH = 16
SEQ = 2048
DEPTH = 2

CTX_LEN = 256
GRID_W = 64
N_BRANCHES = 4
BRANCH_WIDTH = 512
POOL_WIDTH = 512
POOL_GROUP = 128
POOL_WINDOWS = (2, 4, 8, 16)
SSD_HEADS = 8
SSD_HEADDIM = 64
SSD_INNER = SSD_HEADS * SSD_HEADDIM
SSD_GROUPS = 2
SSD_STATE = 64
SSD_XBC = SSD_INNER + 2 * SSD_GROUPS * SSD_STATE
SSD_CONV = 5
SSD_CHUNK = 128
DIFF_HEADS = 4
DIFF_DK = 64
DIFF_DV = 2 * DIFF_DK
DIFF_QK_WIDTH = DIFF_HEADS * 2 * DIFF_DK
DIFF_V_WIDTH = DIFF_HEADS * DIFF_DV
ATTN_BLOCK = 128
ROPE_BASE = 10000.0
SGU_WIDTH = 512
SGU_GROUPS = 4
SGU_GROUP_W = SGU_WIDTH // SGU_GROUPS
SGU_CHUNK = 128
N_EXPERTS = 16
EXPERT_FF = 512
EC_CAPACITY_FACTOR = 2
LN_EPS = 1e-5
ALPHA = (2 * DEPTH) ** 0.25
BETA = (8 * DEPTH) ** -0.25
PROJ_WIDTHS = (POOL_WIDTH, SSD_INNER, SSD_XBC, 2 * SSD_HEADS, DIFF_QK_WIDTH, DIFF_QK_WIDTH, DIFF_V_WIDTH, 2 * SGU_WIDTH)
IN_COLS = sum(PROJ_WIDTHS)

kernel_name = 'hybrid_parallel_mixer_ec_moe_diffusion'


def layer_norm(x, g, b):
    xf = x.astype(jnp.float32)
    mu = jnp.mean(xf, axis=-1, keepdims=True)
    var = jnp.mean(jnp.square(xf - mu), axis=-1, keepdims=True)
    return ((xf - mu) * lax.rsqrt(var + LN_EPS) * g + b).astype(x.dtype)


def rms_norm(xf, g):
    return xf * lax.rsqrt(jnp.mean(jnp.square(xf), axis=-1, keepdims=True) + LN_EPS) * g


def modulate(x, shift, scale):
    return x * (1 + scale) + shift


def flip(t):
    return jnp.flip(t, axis=1)


def pool_mix(xp, pool_w, pool_scale):
    bsz, n, _ = xp.shape
    xf = xp.astype(jnp.float32)
    cs = jnp.concatenate([jnp.zeros((bsz, 1, POOL_WIDTH), jnp.float32), jnp.cumsum(xf, axis=1)], axis=1)
    t = jnp.arange(n)
    outs = []
    for g, w in enumerate(POOL_WINDOWS):
        lo = jnp.clip(t - w // 2, 0, n)
        hi = jnp.clip(t + w // 2, 0, n)
        csg = cs[:, :, g * POOL_GROUP:(g + 1) * POOL_GROUP]
        mean = (csg[:, hi] - csg[:, lo]) / (hi - lo).astype(jnp.float32)[None, :, None]
        outs.append(mean - xf[:, :, g * POOL_GROUP:(g + 1) * POOL_GROUP])
    pooled = jnp.stack(outs, axis=2).astype(xp.dtype)
    mixed = jnp.einsum('bngc,gcd->bngd', pooled, pool_w).reshape(bsz, n, POOL_WIDTH)
    return mixed * pool_scale


def depthwise_conv(x, w, b):
    ch = x.shape[-1]
    y = lax.conv_general_dilated(x, w[:, None, :].astype(x.dtype), window_strides=(1,),
                                 padding=[(SSD_CONV // 2, SSD_CONV // 2)],
                                 dimension_numbers=('NWC', 'WIO', 'NWC'), feature_group_count=ch)
    return y + b


def ssd_prep(xbc, dt_raw, conv_w, conv_b, dt_bias):
    bsz, n, _ = xbc.shape
    xbc = jax.nn.silu(depthwise_conv(xbc, conv_w, conv_b))
    xs, bm, cm = jnp.split(xbc, [SSD_INNER, SSD_INNER + SSD_GROUPS * SSD_STATE], axis=-1)
    dt = jax.nn.softplus(dt_raw.astype(jnp.float32).reshape(bsz, n, 2, SSD_HEADS) + dt_bias.astype(jnp.float32))
    return (xs.reshape(bsz, n, SSD_HEADS, SSD_HEADDIM),
            bm.reshape(bsz, n, SSD_GROUPS, SSD_STATE),
            cm.reshape(bsz, n, SSD_GROUPS, SSD_STATE),
            dt[:, :, 0], dt[:, :, 1])


def ssd_chunked(x, dt, a_neg, bmat, cmat, init_state, with_output):
    f32 = jnp.float32
    bsz, n, nh, hp = x.shape
    nc = n // SSD_CHUNK
    rep = nh // SSD_GROUPS
    xdt = (x.astype(f32) * dt[..., None]).reshape(bsz, nc, SSD_CHUNK, nh, hp)
    a_cs = jnp.cumsum((dt * a_neg).reshape(bsz, nc, SSD_CHUNK, nh).transpose(0, 3, 1, 2), axis=-1)
    bh = jnp.repeat(bmat.astype(f32), rep, axis=2).reshape(bsz, nc, SSD_CHUNK, nh, SSD_STATE)
    ch = jnp.repeat(cmat.astype(f32), rep, axis=2).reshape(bsz, nc, SSD_CHUNK, nh, SSD_STATE)
    decay_to_end = jnp.exp(a_cs[..., -1:] - a_cs)
    chunk_states = jnp.einsum('bclhn,bhcl,bclhp->bchpn', bh, decay_to_end, xdt)
    chunk_decay = jnp.exp(a_cs[..., -1])

    def carry_step(state, inp):
        s_k, d_k = inp
        return state * d_k[:, :, None, None] + s_k, state

    final, entering = lax.scan(carry_step, init_state,
                               (jnp.moveaxis(chunk_states, 1, 0), jnp.moveaxis(chunk_decay, 2, 0)))
    if not with_output:
        return None, final
    entering = jnp.moveaxis(entering, 0, 1)
    seg = a_cs[..., :, None] - a_cs[..., None, :]
    lower = jnp.tril(jnp.ones((SSD_CHUNK, SSD_CHUNK), dtype=bool))
    lmat = jnp.exp(jnp.where(lower, seg, -jnp.inf))
    scores = jnp.einsum('bclhn,bcshn->bhcls', ch, bh) * lmat
    y = (jnp.einsum('bhcls,bcshp->bclhp', scores, xdt)
         + jnp.einsum('bclhn,bchpn,bhcl->bclhp', ch, entering, jnp.exp(a_cs)))
    return y.reshape(bsz, n, nh, hp), final


def ssd_finish(y_f, y_b, xs, z, d_skip, g):
    bsz, n = z.shape[:2]
    y = y_f + y_b + d_skip.astype(jnp.float32)[:, None] * xs.astype(jnp.float32)
    y = y.reshape(bsz, n, SSD_INNER) * jax.nn.silu(z.astype(jnp.float32))
    return rms_norm(y, g.astype(jnp.float32)).astype(z.dtype)


def rotate(x, pos):
    half = x.shape[-1] // 2
    inv = ROPE_BASE ** (-jnp.arange(half, dtype=jnp.float32) / half)
    ang = pos.astype(jnp.float32)[:, None] * inv
    cos = jnp.cos(ang)[None, :, None, :]
    sin = jnp.sin(ang)[None, :, None, :]
    xf = x.astype(jnp.float32)
    x1, x2 = xf[..., :half], xf[..., half:]
    return jnp.concatenate([x1 * cos - x2 * sin, x1 * sin + x2 * cos], axis=-1).astype(x.dtype)


def axial_rope(x, rows, cols):
    half = x.shape[-1] // 2
    return jnp.concatenate([rotate(x[..., :half], rows), rotate(x[..., half:], cols)], axis=-1)


def diff_heads(q, rows, cols, use_rope):
    bsz, n, _ = q.shape
    q = q.reshape(bsz, n, DIFF_HEADS * 2, DIFF_DK)
    if use_rope:
        q = axial_rope(q, rows, cols)
    q = q.reshape(bsz, n, DIFF_HEADS, 2, DIFF_DK).transpose(0, 2, 3, 1, 4)
    return q[:, :, 0], q[:, :, 1]


def value_heads(v):
    bsz, n, _ = v.shape
    return v.reshape(bsz, n, DIFF_HEADS, DIFF_DV).transpose(0, 2, 1, 3)


def diff_attend(q1, q2, k1, k2, v, lam):
    scale = DIFF_DK ** -0.5
    s1 = jnp.einsum('bhqd,bhkd->bhqk', q1, k1).astype(jnp.float32) * scale
    s2 = jnp.einsum('bhqd,bhkd->bhqk', q2, k2).astype(jnp.float32) * scale
    a = jax.nn.softmax(s1, axis=-1) - lam * jax.nn.softmax(s2, axis=-1)
    return jnp.einsum('bhqk,bhkd->bhqd', a.astype(v.dtype), v)


def blocked_diff_attention(q1, q2, k1, k2, v, lam):
    bsz, nh, n, dk = q1.shape
    nb = n // ATTN_BLOCK

    def blocks(q):
        return q.reshape(bsz, nh, nb, ATTN_BLOCK, dk).transpose(2, 0, 1, 3, 4)

    out = lax.map(lambda qq: diff_attend(qq[0], qq[1], k1, k2, v, lam), (blocks(q1), blocks(q2)))
    return out.transpose(1, 2, 0, 3, 4).reshape(bsz, nh, n, DIFF_DV)


def diff_head_norm(o, g, lam_init):
    bsz, nh, n, dv = o.shape
    of = rms_norm(o.astype(jnp.float32), g.astype(jnp.float32)) * (1.0 - lam_init)
    return of.transpose(0, 2, 1, 3).reshape(bsz, n, nh * dv).astype(o.dtype)


def spatial_gating(uv, ln_g, ln_b, w_s, b_s):
    bsz, n, _ = uv.shape
    u, v = jnp.split(jax.nn.gelu(uv), 2, axis=-1)
    v = layer_norm(v, ln_g, ln_b)
    vg = v.reshape(bsz, n // SGU_CHUNK, SGU_CHUNK, SGU_GROUPS, SGU_GROUP_W)
    vm = jnp.einsum('gpq,bcqgd->bcpgd', w_s, vg) + b_s.T[None, None, :, :, None]
    return u * vm.reshape(bsz, n, SGU_WIDTH)


def merge_branches(h, branches, w_gate, w_branch, w_out):
    acc = jax.nn.sigmoid(h @ w_gate[0]) * (branches[0] @ w_branch[0])
    for k in range(1, N_BRANCHES):
        acc = acc + jax.nn.sigmoid(h @ w_gate[k]) * (branches[k] @ w_branch[k])
    return acc @ w_out


def token_mixer(h_lat, h_ctx, rows, cols, layer_idx, need_ctx, w_in, conv_w, conv_b, a_log, dt_bias,
                ssd_d, ssd_norm_g, diff_lambda, diff_norm_g, pool_w, pool_scale, sgu_ln_g, sgu_ln_b,
                sgu_w, sgu_b, w_gate, w_branch, w_out):
    offsets = np.cumsum(PROJ_WIDTHS)[:-1].tolist()
    pool_l, z_l, xbc_l, dt_l, q_l, k_l, v_l, uv_l = jnp.split(h_lat @ w_in, offsets, axis=-1)
    pool_c, z_c, xbc_c, dt_c, q_c, k_c, v_c, uv_c = jnp.split(h_ctx @ w_in, offsets, axis=-1)

    y_pool_l = pool_mix(pool_l, pool_w, pool_scale)

    a_neg = -jnp.exp(a_log.astype(jnp.float32))
    xs_c, bm_c, cm_c, dtf_c, dtb_c = ssd_prep(xbc_c, dt_c, conv_w, conv_b, dt_bias)
    zero = jnp.zeros((h_ctx.shape[0], SSD_HEADS, SSD_HEADDIM, SSD_STATE), jnp.float32)
    yf_c, sf_c = ssd_chunked(xs_c, dtf_c, a_neg[0], bm_c, cm_c, zero, need_ctx)
    yb_c, sb_c = ssd_chunked(flip(xs_c), flip(dtb_c), a_neg[1], flip(bm_c), flip(cm_c), zero, need_ctx)
    xs_l, bm_l, cm_l, dtf_l, dtb_l = ssd_prep(xbc_l, dt_l, conv_w, conv_b, dt_bias)
    yf_l, _ = ssd_chunked(xs_l, dtf_l, a_neg[0], bm_l, cm_l, sf_c, True)
    yb_l, _ = ssd_chunked(flip(xs_l), flip(dtb_l), a_neg[1], flip(bm_l), flip(cm_l), sb_c, True)
    y_ssd_l = ssd_finish(yf_l, flip(yb_l), xs_l, z_l, ssd_d, ssd_norm_g)

    lam_init = 0.8 - 0.6 * math.exp(-0.3 * layer_idx)
    dl = diff_lambda.astype(jnp.float32)
    lam = jnp.exp(jnp.sum(dl[0] * dl[1])) - jnp.exp(jnp.sum(dl[2] * dl[3])) + lam_init
    q1_l, q2_l = diff_heads(q_l, rows, cols, True)
    k1_l, k2_l = diff_heads(k_l, rows, cols, True)
    k1_c, k2_c = diff_heads(k_c, rows, cols, False)
    vh_l, vh_c = value_heads(v_l), value_heads(v_c)
    k1_all = jnp.concatenate([k1_l, k1_c], axis=2)
    k2_all = jnp.concatenate([k2_l, k2_c], axis=2)
    v_all = jnp.concatenate([vh_l, vh_c], axis=2)
    y_diff_l = diff_head_norm(blocked_diff_attention(q1_l, q2_l, k1_all, k2_all, v_all, lam), diff_norm_g, lam_init)

    y_sgu_l = spatial_gating(uv_l, sgu_ln_g, sgu_ln_b, sgu_w, sgu_b)

    y_lat = merge_branches(h_lat, (y_pool_l, y_ssd_l, y_diff_l, y_sgu_l), w_gate, w_branch, w_out)
    if not need_ctx:
        return y_lat, None

    y_pool_c = pool_mix(pool_c, pool_w, pool_scale)
    y_ssd_c = ssd_finish(yf_c, flip(yb_c), xs_c, z_c, ssd_d, ssd_norm_g)
    q1_c, q2_c = diff_heads(q_c, rows, cols, False)
    y_diff_c = diff_head_norm(diff_attend(q1_c, q2_c, k1_c, k2_c, vh_c, lam), diff_norm_g, lam_init)
    y_sgu_c = spatial_gating(uv_c, sgu_ln_g, sgu_ln_b, sgu_w, sgu_b)
    y_ctx = merge_branches(h_ctx, (y_pool_c, y_ssd_c, y_diff_c, y_sgu_c), w_gate, w_branch, w_out)
    return y_lat, y_ctx


def expert_choice_ffn(h, w_router, w1, w3, w2):
    bsz, n, d = h.shape
    cap = EC_CAPACITY_FACTOR * n // N_EXPERTS
    aff = jax.nn.softmax((h @ w_router).astype(jnp.float32), axis=-1)
    gate, idx = lax.top_k(jnp.swapaxes(aff, 1, 2), cap)
    xe = jax.vmap(lambda hb, ib: hb[ib])(h, idx)
    hid = jax.nn.silu(jnp.einsum('becd,edf->becf', xe, w1)) * jnp.einsum('becd,edf->becf', xe, w3)
    ye = jnp.einsum('becf,efd->becd', hid, w2) * gate[..., None].astype(h.dtype)
    return jax.vmap(lambda ib, yb: jnp.zeros((n, d), yb.dtype).at[ib.reshape(-1)].add(yb.reshape(-1, d)))(idx, ye)


def setup_inputs(seed: int = 0) -> dict:
    key = jax.random.key(seed)
    keys = jax.random.split(key, 40)
    counter = [0]

    def nxt():
        k = keys[counter[0]]
        counter[0] += 1
        return k

    f32 = jnp.float32

    def nrm(shape, scale):
        return jax.random.normal(nxt(), shape, f32) * scale

    L, D = DEPTH, D_MODEL
    dt0 = jnp.exp(jax.random.uniform(nxt(), (L, 2, SSD_HEADS), f32, math.log(1e-3), math.log(1e-1)))
    return {
        'x': nrm((BATCH, SEQ, D), 1.0),
        'c': nrm((BATCH, D), 1.0),
        'ctx': nrm((BATCH, CTX_LEN, D), 1.0),
        'c_ctx': nrm((D,), 1.0),
        'w_mod': nrm((L, D, 6 * D), 0.5 * D ** -0.5),
        'b_mod': nrm((L, 6 * D), 0.01),
        'w_in': nrm((L, D, IN_COLS), D ** -0.5),
        'conv_w': nrm((L, SSD_CONV, SSD_XBC), SSD_CONV ** -0.5),
        'conv_b': nrm((L, SSD_XBC), 0.01),
        'a_log': jnp.log(jax.random.uniform(nxt(), (L, 2, SSD_HEADS), f32, 1.0, 16.0)),
        'dt_bias': dt0 + jnp.log(-jnp.expm1(-dt0)),
        'ssd_d': 1.0 + nrm((L, SSD_HEADS), 0.1),
        'ssd_norm_g': 1.0 + nrm((L, SSD_INNER), 0.02),
        'diff_lambda': nrm((L, 4, DIFF_DK), 0.1),
        'diff_norm_g': 1.0 + nrm((L, DIFF_DV), 0.02),
        'pool_w': nrm((L, len(POOL_WINDOWS), POOL_GROUP, POOL_GROUP), POOL_GROUP ** -0.5),
        'pool_scale': 1.0 + nrm((L, POOL_WIDTH), 0.1),
        'sgu_ln_g': 1.0 + nrm((L, SGU_WIDTH), 0.02),
        'sgu_ln_b': nrm((L, SGU_WIDTH), 0.01),
        'sgu_w': nrm((L, SGU_GROUPS, SGU_CHUNK, SGU_CHUNK), SGU_CHUNK ** -0.5),
        'sgu_b': 1.0 + nrm((L, SGU_GROUPS, SGU_CHUNK), 0.1),
        'w_gate': nrm((L, N_BRANCHES, D, D), D ** -0.5),
        'w_branch': nrm((L, N_BRANCHES, BRANCH_WIDTH, D), BRANCH_WIDTH ** -0.5 * BETA),
        'w_out': nrm((L, D, D), D ** -0.5 * BETA),
        'ln1_g': 1.0 + nrm((L, D), 0.02),
        'ln1_b': nrm((L, D), 0.01),
        'w_router': nrm((L, D, N_EXPERTS), D ** -0.5),
        'w1': nrm((L, N_EXPERTS, D, EXPERT_FF), D ** -0.5),
        'w3': nrm((L, N_EXPERTS, D, EXPERT_FF), D ** -0.5),
        'w2': nrm((L, N_EXPERTS, EXPERT_FF, D), EXPERT_FF ** -0.5 * BETA),
        'ln2_g': 1.0 + nrm((L, D), 0.02),
        'ln2_b': nrm((L, D), 0.01),
    }


def reference(x, c, ctx, c_ctx, w_mod, b_mod, w_in, conv_w, conv_b, a_log, dt_bias, ssd_d, ssd_norm_g,
              diff_lambda, diff_norm_g, pool_w, pool_scale, sgu_ln_g, sgu_ln_b, sgu_w, sgu_b, w_gate,
              w_branch, w_out, ln1_g, ln1_b, w_router, w1, w3, w2, ln2_g, ln2_b):
    n_lat = x.shape[1]
    n_rows = n_lat // GRID_W
    rows = jnp.broadcast_to(jnp.arange(n_rows, dtype=jnp.int32)[:, None], (n_rows, GRID_W)).reshape(-1)
    cols = jnp.broadcast_to(jnp.arange(GRID_W, dtype=jnp.int32)[None, :], (n_rows, GRID_W)).reshape(-1)
    for l in range(DEPTH):
        last = l == DEPTH - 1
        m_lat = jnp.split((jax.nn.silu(c) @ w_mod[l] + b_mod[l])[:, None, :], 6, axis=-1)
        m_ctx = jnp.split((jax.nn.silu(c_ctx) @ w_mod[l] + b_mod[l]).reshape(1, 1, -1), 6, axis=-1)
        h_lat = modulate(x, m_lat[0], m_lat[1])
        h_ctx = modulate(ctx, m_ctx[0], m_ctx[1])
        y_lat, y_ctx = token_mixer(h_lat, h_ctx, rows, cols, l, not last, w_in[l], conv_w[l], conv_b[l],
                                   a_log[l], dt_bias[l], ssd_d[l], ssd_norm_g[l], diff_lambda[l],
                                   diff_norm_g[l], pool_w[l], pool_scale[l], sgu_ln_g[l], sgu_ln_b[l],
                                   sgu_w[l], sgu_b[l], w_gate[l], w_branch[l], w_out[l])
        x = layer_norm(ALPHA * x + m_lat[2] * y_lat, ln1_g[l], ln1_b[l])
        y_ffn = expert_choice_ffn(modulate(x, m_lat[3], m_lat[4]), w_router[l], w1[l], w3[l], w2[l])
        x = layer_norm(ALPHA * x + m_lat[5] * y_ffn, ln2_g[l], ln2_b[l])
        if not last:
            ctx = layer_norm(ALPHA * ctx + m_ctx[2] * y_ctx, ln1_g[l], ln1_b[l])
            y_ffn_c = expert_choice_ffn(modulate(ctx, m_ctx[3], m_ctx[4]), w_router[l], w1[l], w3[l], w2[l])
            ctx = layer_norm(ALPHA * ctx + m_ctx[5] * y_ffn_c, ln2_g[l], ln2_b[l])
    return x
```

```python
import math
from contextlib import ExitStack
import numpy as np
import concourse.bass as bass
import concourse.mybir as mybir
from concourse.alu_op_type import AluOpType as ALU
from concourse.bass_utils import run_bass_kernel_spmd

AF = mybir.ActivationFunctionType
AX = mybir.AxisListType
F32 = mybir.dt.float32
BF16 = mybir.dt.bfloat16

ENGS = ('pe', 'dve', 'act', 'pool', 'sp')

D = 1024
NL = 2048
NCX = 256
NT = NL + NCX
NTILES = NT // 128
INC = 4368
LN_EPS = 1e-5
ALPHA = 4.0 ** 0.25
C_POOL, C_Z, C_XBC, C_DT, C_Q, C_K, C_V, C_U, C_SV = 0, 512, 1024, 1792, 1808, 2320, 2832, 3344, 3856


class H:
    __slots__ = ('w', 'r')

    def __init__(self):
        self.w = None
        self.r = {}


class Tl:
    __slots__ = ('t', 'h')

    def __init__(self, t):
        self.t = t
        self.h = H()


class Rot:
    def __init__(self, tiles):
        self.tiles = tiles
        self.i = 0

    def next(self):
        t = self.tiles[self.i % len(self.tiles)]
        self.i += 1
        return t


class Prog:
    def __init__(self, nc):
        self.nc = nc
        self.ops = {e: [] for e in ENGS}
        self.waited = {e: {} for e in ENGS}
        self.dma_cnt = {}
        self.dma_last = {}
        self.targets = set()
        self.ges = ExitStack()
        self.es = self.ges
        self.sems = {}
        self.val = {}
        self.inc_count = {e: 0 for e in ENGS}
        self.ops_total = {e: 0 for e in ENGS}
        self.n_inst = 0
        self.uid = 0

    def sbuf(self, shape, dtype, name=None):
        self.uid += 1
        return Tl(self.es.enter_context(self.nc.sbuf_tensor('%s_%d' % (name or 'sb', self.uid), list(shape), dtype)))

    def rot(self, n, shape, dtype, name=None):
        return Rot([self.sbuf(shape, dtype, name) for _ in range(n)])

    def _need(self, eng, tok, waits):
        if tok is None:
            return
        key, idx = tok
        if key == eng and eng == 'pe':
            return
        if self.waited[eng].get(key, -1) >= idx:
            return
        self.waited[eng][key] = idx
        waits.append(tok)
        if key in ENGS:
            self.targets.add(tok)

    def _deps(self, eng, reads, writes):
        waits = []
        for t in reads:
            self._need(eng, t.h.w, waits)
        for t in writes:
            self._need(eng, t.h.w, waits)
            for k, i in t.h.r.items():
                self._need(eng, (k, i), waits)
        return waits

    def _mark(self, tok, reads, writes):
        key, idx = tok
        for t in reads:
            if t.h.r.get(key, -1) < idx:
                t.h.r[key] = idx
        for t in writes:
            t.h.w = tok
            t.h.r = {}

    def op(self, eng, fn, reads=(), writes=()):
        waits = self._deps(eng, reads, writes)
        idx = self.ops_total[eng]
        self.ops_total[eng] += 1
        tok = (eng, idx)
        self.ops[eng].append([waits, fn, tok, False])
        self._mark(tok, reads, writes)
        return tok

    def dma(self, eng, key, pairs, reads=(), writes=(), **kw):
        key = eng + '_' + key
        waits = self._deps(eng, reads, writes)
        self._need(eng, self.dma_last.get(key), waits)
        cnt = self.dma_cnt.get(key, 0) + len(pairs)
        self.dma_cnt[key] = cnt
        tok = (('d', key), cnt)
        self.dma_last[key] = tok

        def fn(e, pairs=pairs, kw=kw):
            return [e.dma_start(out=o, in_=i, **kw) for (o, i) in pairs]
        self.ops[eng].append([waits, fn, tok, True])
        self._mark(tok, reads, writes)
        return tok

    def begin(self):
        self.pes = ExitStack()
        self.es = self.pes

    def barrier(self):
        for e in ENGS:
            waits = []
            for f in ENGS:
                if f != e and self.ops_total[f] > 0:
                    self._need(e, (f, self.ops_total[f] - 1), waits)
            for key, tok in self.dma_last.items():
                self._need(e, tok, waits)
            if waits:
                self.ops[e].append([waits, None, None, False])

    def end(self):
        self.barrier()
        self._emit_block()
        self.es = self.ges
        self.pes.close()

    def _emit_block(self):
        nc = self.nc
        for e in ENGS:
            if e not in self.sems:
                self.sems[e] = self.ges.enter_context(nc.semaphore('s_' + e))
        for key in self.dma_cnt:
            if ('d', key) not in self.sems:
                self.sems[('d', key)] = self.ges.enter_context(nc.semaphore('d_%s' % (str(key),)))
        sems = self.sems
        val = self.val
        for e in ENGS:
            for op in self.ops[e]:
                tok = op[2]
                if tok is not None and tok[0] == e and tok in self.targets:
                    self.inc_count[e] += 1
                    val[tok] = self.inc_count[e]

        def tokval(tok):
            key, idx = tok
            return val[tok] if key in ENGS else 16 * idx

        engmap = {'pe': 'tensor', 'dve': 'vector', 'act': 'scalar', 'pool': 'gpsimd', 'sp': 'sync'}
        ops = self.ops
        with nc.Block() as block:
            def make(ename):
                def body(e):
                    for waits, fn, tok, isdma in ops[ename]:
                        for w in waits:
                            e.wait_ge(sems[w[0]], tokval(w))
                        if fn is None:
                            continue
                        r = fn(e)
                        if isdma:
                            for inst in r:
                                inst.then_inc(sems[tok[0]], 16)
                        elif tok in val:
                            r.then_inc(sems[ename], 1)
                return body
            for ename in ENGS:
                getattr(block, engmap[ename])(make(ename))
        self.n_inst += sum(len(v) for v in ops.values())
        self.ops = {e: [] for e in ENGS}

    def close(self):
        self.ges.close()


def MM(P, out, lhsT, rhs, start, stop, rd, wr):
    return P.op('pe', lambda e: e.matmul(out, lhsT, rhs, start=start, stop=stop), rd, wr)


def TR(P, out, in_, ident, rd, wr):
    return P.op('pe', lambda e: e.transpose(out, in_, ident), rd, wr)


def ACT(P, out, in_, func, rd, wr, scale=None, bias=None, accum=None):
    kw = {}
    if scale is not None:
        kw['scale'] = scale
    if bias is not None:
        kw['bias'] = bias
    if accum is not None:
        kw['accum_out'] = accum
    return P.op('act', lambda e: e.activation(out=out, in_=in_, func=func, **kw), rd, wr)


def TT(P, out, a, b, op, rd, wr, eng='dve'):
    return P.op(eng, lambda e: e.tensor_tensor(out=out, in0=a, in1=b, op=op), rd, wr)


def TS(P, out, a, s1, op0, rd, wr, s2=None, op1=None, eng='dve'):
    if op1 is None:
        return P.op(eng, lambda e: e.tensor_scalar(out=out, in0=a, scalar1=s1, scalar2=None, op0=op0), rd, wr)
    return P.op(eng, lambda e: e.tensor_scalar(out=out, in0=a, scalar1=s1, scalar2=s2, op0=op0, op1=op1), rd, wr)


def STT(P, out, in0, scalar, in1, op0, op1, rd, wr):
    return P.op('dve', lambda e: e.scalar_tensor_tensor(out=out, in0=in0, scalar=scalar, in1=in1, op0=op0, op1=op1), rd, wr)


def CP(P, out, in_, rd, wr, eng='dve'):
    if eng == 'act':
        return P.op('act', lambda e: e.copy(out=out, in_=in_), rd, wr)
    return P.op(eng, lambda e: e.tensor_copy(out=out, in_=in_), rd, wr)


def RECIP(P, out, in_, rd, wr):
    return P.op('dve', lambda e: e.reciprocal(out=out, in_=in_), rd, wr)


def MEMSET(P, ap, v, wr, eng='dve'):
    return P.op(eng, lambda e: e.memset(ap, v), (), wr)


def make_consts():
    c = {}
    k = np.arange(128)
    c['ident'] = np.eye(128, dtype=np.float32)
    c['ones'] = np.ones((128, 128), np.float32)
    tm = (k[:, None] <= k[None, :]).astype(np.float32)
    c['tm'] = tm
    c['tmT'] = np.ascontiguousarray(tm.T)
    g = (k[:, None] > k[None, :]).astype(np.float32)
    c['g'] = g
    c['gT'] = np.ascontiguousarray(g.T)
    c['negf'] = np.where(k[:, None] <= k[None, :], 0.0, -30000.0).astype(np.float32)
    c['negb'] = np.where(k[:, None] >= k[None, :], 0.0, -30000.0).astype(np.float32)
    partner = np.where((k % 32) < 16, k + 16, k - 16)
    perm = np.zeros((128, 128), np.float32)
    perm[partner, k] = 1.0
    c['perm'] = perm
    t = np.arange(NL)
    rowp = (t // 64).astype(np.float32)
    colp = (t % 64).astype(np.float32)
    inv = (10000.0 ** (-np.arange(16, dtype=np.float32) / 16.0)).astype(np.float32)
    cos = np.zeros((128, NL), np.float32)
    sin = np.zeros((128, NL), np.float32)
    for r in range(128):
        d = r % 64
        pos = rowp if d < 32 else colp
        dd = d % 32
        ang = (pos * inv[dd % 16]).astype(np.float32)
        cos[r] = np.cos(ang)
        sin[r] = -np.sin(ang) if dd < 16 else np.sin(ang)
    c['cos'] = cos
    c['sin'] = sin
    for nm, n in (('rcl', NL), ('rcc', NCX)):
        tt = np.arange(n)
        rc = np.zeros((4, n), np.float32)
        for gi, w in enumerate((2, 4, 8, 16)):
            lo = np.clip(tt - w // 2, 0, n)
            hi = np.clip(tt + w // 2, 0, n)
            rc[gi] = 1.0 / (hi - lo).astype(np.float32)
        c[nm] = rc
    c['iota'] = np.tile(np.arange(256, dtype=np.float32)[None, :], (128, 1))
    c['slotid'] = (np.arange(128, dtype=np.float32)[:, None] + 128.0 * np.arange(2, dtype=np.float32)[None, :]).astype(np.float32)
    es = np.zeros((16, 16, 128), np.float32)
    for e in range(16):
        es[e, e, :] = 1.0
    c['esel'] = es
    return c


CONST_SHAPES = {'ident': (128, 128), 'ones': (128, 128), 'tm': (128, 128), 'tmT': (128, 128), 'g': (128, 128), 'gT': (128, 128),
                'negf': (128, 128), 'negb': (128, 128), 'perm': (128, 128), 'cos': (128, NL), 'sin': (128, NL),
                'rcl': (4, NL), 'rcc': (4, NCX), 'iota': (128, 256), 'slotid': (128, 2), 'esel': (16, 16, 128)}

INPUT_SHAPES = {
    'x': (2, NL, D), 'ctx': (2, NCX, D), 'c3': (4, D),
    'w_mod': (2, D, 6 * D), 'b_mod': (2, 6 * D), 'w_in': (2, D, INC), 'conv_w': (2, 5, 768), 'conv_b': (2, 768),
    'a_log': (2, 16), 'dt_bias': (2, 16), 'ssd_d': (2, 8), 'ssd_norm_g': (2, 512), 'diff_lambda': (2, 4, 64),
    'diff_norm_g': (2, 128), 'pool_w': (2, 4, 128, 128), 'pool_scale': (2, 512), 'sgu_ln_g': (2, 512), 'sgu_ln_b': (2, 512),
    'sgu_w': (2, 4, 128, 128), 'sgu_b': (2, 4, 128), 'w_gate': (2, 4, D, D), 'w_branch': (2, 4, 512, D), 'w_out': (2, D, D),
    'ln1_g': (2, D), 'ln1_b': (2, D), 'w_router': (2, D, 16), 'w1': (2, 16, D, 512), 'w3': (2, 16, D, 512), 'w2': (2, 16, 512, D),
    'ln2_g': (2, D), 'ln2_b': (2, D),
}

SCRATCH = {
    'modrow': ((2, 4, 6 * D), F32), 'xc': ((2, NT, D), F32), 'xm': ((NT, D), F32),
    'poolT': ((512, NT), BF16), 'zt': ((NT, 512), BF16), 'xbcT': ((768, NT), BF16), 'dtb': ((NT, 16), F32), 'adtb': ((NT, 16), F32),
    'qT': ((512, NT), BF16), 'kT': ((512, NT), BF16), 'vtok': ((NT, 512), BF16), 'uT': ((512, NT), BF16), 'svn': ((NT, 512), BF16),
    'xactT': ((768, NT), BF16), 'xtok': ((NT, 640), BF16), 'yf': ((NT, 512), F32), 'ybrT': ((4, 512, NT), BF16),
    'accT': ((D, NT), BF16), 'hftok': ((NT, D), BF16), 'hTd': ((D, NT), BF16), 'yffn': ((NT, D), F32),
}


class K:
    pass


def build(debug=None):
    nc = bass.Bass("TRN2", target_bir_lowering=False)
    dbg = debug or {}
    layers = dbg.get('layers', [0, 1])
    samples = dbg.get('samples', [0, 1])
    stop = dbg.get('stop', None)
    outs = dbg.get('outs', [])
    I = {}
    for nme, shp in INPUT_SHAPES.items():
        I[nme] = nc.dram_tensor(nme, list(shp), F32, kind="ExternalInput").ap()
    C = {}
    for nme, shp in CONST_SHAPES.items():
        C[nme] = nc.dram_tensor('c_' + nme, list(shp), F32, kind="ExternalInput").ap()
    S = {}
    for nme, (shp, dt) in SCRATCH.items():
        kind = "ExternalOutput" if nme in outs else "Internal"
        S[nme] = nc.dram_tensor('s_' + nme, list(shp), dt, kind=kind).ap()
    OUT = nc.dram_tensor("out", [2, NL, D], F32, kind="ExternalOutput").ap()
    P = Prog(nc)
    k = K()
    k.nc, k.P, k.I, k.C, k.S, k.OUT = nc, P, I, C, S, OUT
    k.hS = {n: Tl(None) for n in SCRATCH}
    k.hOUT = Tl(None)
    k.PS = [Tl(P.ges.enter_context(nc.psum_tensor('ps%d' % i, [128, 512], F32))) for i in range(8)]
    k.c = {}
    for nme in ('ident', 'ones', 'tm', 'tmT', 'g', 'gT', 'negf', 'negb', 'perm'):
        k.c[nme] = P.sbuf([128, 128], F32, 'c' + nme)
    k.c['iota'] = P.sbuf([128, 256], F32, 'ciota')
    k.c['slotid'] = P.sbuf([128, 2], F32, 'cslot')
    k.c['esel'] = P.sbuf([16, 16, 128], F32, 'cesel')
    k.c['identb'] = P.sbuf([128, 128], BF16, 'cidb')
    k.c['onesb'] = P.sbuf([128, 128], BF16, 'conb')
    k.modc = [P.sbuf([128, 48, 4], F32, 'modc%d' % l) for l in range(2)]
    k.affT = P.sbuf([16, NT], F32, 'affT')
    k.posmT = P.sbuf([16, NT], F32, 'posmT')
    k.posk = P.sbuf([128, NTILES, 16], F32, 'posk')
    k.gatek = P.sbuf([128, NTILES, 16], F32, 'gatek')

    P.begin()
    for nme in ('ident', 'ones', 'tm', 'tmT', 'g', 'gT', 'negf', 'negb', 'perm', 'iota', 'slotid', 'esel'):
        P.dma('sp', 'cst', [(k.c[nme].t[:], C[nme])], (), [k.c[nme]])
    CP(P, k.c['identb'].t[:], k.c['ident'].t[:], [k.c['ident']], [k.c['identb']])
    CP(P, k.c['onesb'].t[:], k.c['ones'].t[:], [k.c['ones']], [k.c['onesb']])
    P.end()

    phase_mod(k, layers)
    done = False
    for l in layers:
        for s in samples:
            for nm, fn in (('B', phase_inproj), ('C', phase_pool), ('D', phase_ssd), ('E', phase_attn), ('F', phase_sgu),
                           ('G', phase_merge), ('H', phase_out1), ('I', phase_moe)):
                fn(k, l, s)
                if stop == nm:
                    done = True
                    break
            if done:
                break
        if done:
            break
    P.close()
    k.n_inst = P.n_inst
    return nc, k


def phase_mod(k, layers):
    P, I, S, PS = k.P, k.I, k.S, k.PS
    P.begin()
    ident, ones = k.c['ident'], k.c['ones']
    c3 = P.sbuf([4, D], F32, 'c3')
    P.dma('sp', 'ld0', [(c3.t[:], I['c3'])], (), [c3])
    scT = P.sbuf([128, 8, 4], F32, 'scT')
    for kc in range(8):
        TR(P, PS[0].t[:, kc * 4:kc * 4 + 4], c3.t[:, kc * 128:(kc + 1) * 128], ident.t[0:4, 0:4], [c3, ident], [PS[0]])
    for kc in range(8):
        ACT(P, scT.t[:, kc, :], PS[0].t[:, kc * 4:kc * 4 + 4], AF.Silu, [PS[0]], [scT])
    wrot = P.rot(2, [128, 8, 512], F32, 'wm')
    brot = P.rot(2, [1, 512], F32, 'bm')
    for l in layers:
        modr = P.sbuf([4, 6 * D], F32, 'modr')
        for nb in range(12):
            w = wrot.next()
            b = brot.next()
            P.dma('sp', 'wm%d' % (nb % 2), [(w.t[:], I['w_mod'][l, :, nb * 512:(nb + 1) * 512].rearrange("(k p) n -> p k n", p=128))], (), [w])
            P.dma('sp', 'bm%d' % (nb % 2), [(b.t[:], I['b_mod'][l:l + 1, nb * 512:(nb + 1) * 512])], (), [b])
            pr = PS[1 + nb % 2]
            for kc in range(8):
                MM(P, pr.t[0:4, :], scT.t[:, kc, :], w.t[:, kc, :], kc == 0, False, [scT, w], [pr])
            MM(P, pr.t[0:4, :], ones.t[0:1, 0:4], b.t[0:1, :], False, True, [ones, b], [pr])
            CP(P, modr.t[:, nb * 512:(nb + 1) * 512], pr.t[0:4, :], [pr], [modr])
            pc = PS[3 + nb % 2]
            for jj in range(4):
                j = nb * 4 + jj
                for kc in range(8):
                    MM(P, pc.t[:, jj * 4:jj * 4 + 4], w.t[:, kc, jj * 128:(jj + 1) * 128], scT.t[:, kc, :], kc == 0, False, [scT, w], [pc])
                MM(P, pc.t[:, jj * 4:jj * 4 + 4], b.t[0:1, jj * 128:(jj + 1) * 128], ones.t[0:1, 0:4], False, True, [ones, b], [pc])
            CP(P, k.modc[l].t[:, nb * 4:(nb + 1) * 4, :], pc.t[:, 0:16].rearrange("p (j s) -> p j s", s=4), [pc], [k.modc[l]], eng='act')
        P.dma('pool', 'st0', [(S['modrow'][l], modr.t[:])], [modr], [k.hS['modrow']])
    P.end()


def tokblocks(t0, n):
    out = []
    o = 0
    while o < n:
        m = min(512, n - o)
        out.append((t0 + o, m))
        o += m
    return out


def gelu_tanh(P, dst_ap, src, src_ap, tmp, tmp2, wr, shape_sl):
    t, t2 = tmp.t[shape_sl], tmp2.t[shape_sl]
    TT(P, t, src_ap, src_ap, ALU.mult, [src], [tmp])
    TS(P, t, t, 0.044715, ALU.mult, [tmp], [tmp], s2=1.0, op1=ALU.add)
    TT(P, t, t, src_ap, ALU.mult, [tmp, src], [tmp])
    ACT(P, t2, t, AF.Sigmoid, [tmp], [tmp2], scale=1.5957691216057308)
    TT(P, dst_ap, src_ap, t2, ALU.mult, [src, tmp2], wr)


def phase_inproj(k, l, s):
    P, I, S, PS, c = k.P, k.I, k.S, k.PS, k.c
    P.begin()
    hS = k.hS
    hT = P.sbuf([128, 8, NT], BF16, 'hT')
    win = P.sbuf([128, 8, INC], BF16, 'win')
    for kc in range(8):
        prs = []
        for (a, b) in ((0, 2048), (2048, 4096), (4096, INC)):
            prs.append((win.t[:, kc, a:b], I['w_in'][l, kc * 128:(kc + 1) * 128, a:b]))
        P.dma('pool', 'win%d' % (kc % 2), prs, (), [win])
    cos = P.sbuf([128, NL], F32, 'cos')
    sin = P.sbuf([128, NL], F32, 'sin')
    P.dma('sp', 'ld0', [(cos.t[:], k.C['cos'])], (), [cos])
    P.dma('sp', 'ld1', [(sin.t[:], k.C['sin'])], (), [sin])
    sc1 = P.sbuf([128, 8, 4], F32, 'sc1')
    TS(P, sc1.t[:], k.modc[l].t[:, 8:16, :], 1.0, ALU.add, [k.modc[l]], [sc1])
    lng = P.sbuf([128, 512], F32, 'lng')
    lnb = P.sbuf([128, 512], F32, 'lnb')
    dtbias = P.sbuf([128, 16], F32, 'dtbias')
    aneg = P.sbuf([128, 16], F32, 'aneg')
    P.dma('sp', 'ld0', [(lng.t[:], I['sgu_ln_g'][l].partition_broadcast(128))], (), [lng])
    P.dma('sp', 'ld1', [(lnb.t[:], I['sgu_ln_b'][l].partition_broadcast(128))], (), [lnb])
    P.dma('sp', 'ld0', [(dtbias.t[:], I['dt_bias'][l].partition_broadcast(128))], (), [dtbias])
    P.dma('sp', 'ld1', [(aneg.t[:], I['a_log'][l].partition_broadcast(128))], (), [aneg])
    ACT(P, aneg.t[:], aneg.t[:], AF.Exp, [aneg], [aneg])
    TS(P, aneg.t[:], aneg.t[:], -1.0, ALU.mult, [aneg], [aneg])

    xrot = P.rot(2, [128, D], F32, 'xt')
    for i in range(NTILES):
        xt = xrot.next()
        if l == 0:
            src = I['x'][s, i * 128:(i + 1) * 128, :] if i < 16 else I['ctx'][s, (i - 16) * 128:(i - 15) * 128, :]
            rd = []
        else:
            src = S['xc'][s, i * 128:(i + 1) * 128, :]
            rd = [hS['xc']]
        P.dma('sp', 'xt%d' % (i % 2), [(xt.t[:], src)], rd, [xt])
        col = s if i < 16 else 2
        for kc in range(8):
            pb = PS[(kc // 4) + 2 * (i % 2)]
            TR(P, pb.t[:, (kc % 4) * 128:(kc % 4 + 1) * 128], xt.t[:, kc * 128:(kc + 1) * 128], c['ident'].t[:], [xt, c['ident']], [pb])
        for kc in range(8):
            pb = PS[(kc // 4) + 2 * (i % 2)]
            ACT(P, hT.t[:, kc, i * 128:(i + 1) * 128], pb.t[:, (kc % 4) * 128:(kc % 4 + 1) * 128], AF.Identity, [pb, sc1, k.modc[l]], [hT],
                scale=sc1.t[:, kc, col:col + 1], bias=k.modc[l].t[:, kc, col:col + 1])
    P.dma('pool', 'st0', [(S['hTd'].rearrange("(k p) t -> p k t", p=128), hT.t[:])], [hT], [hS['hTd']])

    srot = P.rot(4, [128, 512], BF16, 'stg')
    frot = P.rot(3, [128, 512], F32, 'f32a')
    grot = P.rot(3, [128, 512], F32, 'f32b')
    hrot = P.rot(3, [128, 512], F32, 'f32c')
    psi = [0]

    def nextps(lo=0, n=4):
        p = PS[lo + psi[0] % n]
        psi[0] += 1
        return p
    blocks = tokblocks(0, NL) + tokblocks(NL, NCX)
    groups = [('pool', C_POOL, 4, 'poolT'), ('xbc', C_XBC, 6, 'xbcT'), ('q', C_Q, 4, 'qT'), ('k', C_K, 4, 'kT'), ('u', C_U, 4, 'uT')]
    sti = [0]
    for (gname, c0, nch, dst) in groups:
        for ch in range(nch):
            for (t0, n) in blocks:
                if gname == 'q' and t0 >= NL and l == 1:
                    continue
                pr = nextps(0, 4)
                for kc in range(8):
                    MM(P, pr.t[:, 0:n], win.t[:, kc, c0 + ch * 128:c0 + (ch + 1) * 128], hT.t[:, kc, t0:t0 + n], kc == 0, kc == 7, [win, hT], [pr])
                st = srot.next()
                if gname in ('pool', 'xbc') or (gname in ('q', 'k') and t0 >= NL):
                    if sti[0] % 2 == 0:
                        CP(P, st.t[:, 0:n], pr.t[:, 0:n], [pr], [st], eng='act')
                    else:
                        CP(P, st.t[:, 0:n], pr.t[:, 0:n], [pr], [st])
                elif gname in ('q', 'k'):
                    qf = frot.next()
                    CP(P, qf.t[:, 0:n], pr.t[:, 0:n], [pr], [qf], eng='act')
                    p2 = nextps(4, 4)
                    MM(P, p2.t[:, 0:n], c['perm'].t[:], qf.t[:, 0:n], True, True, [c['perm'], qf], [p2])
                    t1 = grot.next()
                    t2 = hrot.next()
                    TT(P, t1.t[:, 0:n], qf.t[:, 0:n], cos.t[:, t0:t0 + n], ALU.mult, [qf, cos], [t1])
                    TT(P, t2.t[:, 0:n], p2.t[:, 0:n], sin.t[:, t0:t0 + n], ALU.mult, [p2, sin], [t2])
                    TT(P, st.t[:, 0:n], t1.t[:, 0:n], t2.t[:, 0:n], ALU.add, [t1, t2], [st])
                else:
                    xf = frot.next()
                    CP(P, xf.t[:, 0:n], pr.t[:, 0:n], [pr], [xf], eng='act')
                    gelu_tanh(P, st.t[:, 0:n], xf, xf.t[:, 0:n], grot.next(), hrot.next(), [st], (slice(None), slice(0, n)))
                P.dma('pool', 'sto%d' % (sti[0] % 4), [(S[dst][ch * 128:(ch + 1) * 128, t0:t0 + n], st.t[:, 0:n])], [st], [hS[dst]])
                sti[0] += 1

    st6 = P.rot(2, [128, 8], F32, 'st6')
    mv = P.rot(2, [128, 4], F32, 'mv')
    dts = P.rot(2, [128, 16], F32, 'dts')
    adts = P.rot(2, [128, 16], F32, 'adts')
    for i in range(NTILES):
        for (gname, c0, dst) in (('z', C_Z, 'zt'), ('v', C_V, 'vtok'), ('sv', C_SV, 'svn')):
            pr = nextps(0, 4)
            for kc in range(8):
                MM(P, pr.t[:], hT.t[:, kc, i * 128:(i + 1) * 128], win.t[:, kc, c0:c0 + 512], kc == 0, kc == 7, [win, hT], [pr])
            st = srot.next()
            if gname != 'sv':
                CP(P, st.t[:], pr.t[:], [pr], [st], eng='act')
            else:
                xf = frot.next()
                CP(P, xf.t[:], pr.t[:], [pr], [xf], eng='act')
                ge = frot.next()
                gelu_tanh(P, ge.t[:], xf, xf.t[:], grot.next(), hrot.next(), [ge], (slice(None), slice(None)))
                s6 = st6.next()
                m2 = mv.next()
                P.op('dve', lambda e, s6=s6, ge=ge: e.bn_stats(out=s6.t[:, 0:6], in_=ge.t[:]), [ge], [s6])
                P.op('dve', lambda e, s6=s6, m2=m2: e.bn_aggr(out=m2.t[:, 0:2], in_=s6.t[:, 0:6]), [s6], [m2])
                TS(P, m2.t[:, 2:3], m2.t[:, 1:2], LN_EPS, ALU.add, [m2], [m2])
                ACT(P, m2.t[:, 2:3], m2.t[:, 2:3], AF.Sqrt, [m2], [m2])
                RECIP(P, m2.t[:, 3:4], m2.t[:, 2:3], [m2], [m2])
                xn = grot.next()
                TS(P, xn.t[:], ge.t[:], m2.t[:, 0:1], ALU.subtract, [ge, m2], [xn], s2=m2.t[:, 3:4], op1=ALU.mult)
                TT(P, xn.t[:], xn.t[:], lng.t[:], ALU.mult, [xn, lng], [xn])
                TT(P, st.t[:], xn.t[:], lnb.t[:], ALU.add, [xn, lnb], [st])
            P.dma('pool', 'sto%d' % (sti[0] % 4), [(S[dst][i * 128:(i + 1) * 128, :], st.t[:])], [st], [hS[dst]])
            sti[0] += 1
        pr = nextps(0, 4)
        for kc in range(8):
            MM(P, pr.t[:, 0:16], hT.t[:, kc, i * 128:(i + 1) * 128], win.t[:, kc, C_DT:C_DT + 16], kc == 0, kc == 7, [win, hT], [pr])
        d1 = dts.next()
        a1 = adts.next()
        TT(P, d1.t[:], pr.t[:, 0:16], dtbias.t[:], ALU.add, [pr, dtbias], [d1])
        ACT(P, d1.t[:], d1.t[:], AF.Exp, [d1], [d1])
        ACT(P, d1.t[:], d1.t[:], AF.Ln, [d1], [d1], bias=1.0)
        TT(P, a1.t[:], d1.t[:], aneg.t[:], ALU.mult, [d1, aneg], [a1])
        P.dma('pool', 'sto%d' % (sti[0] % 4), [(S['dtb'][i * 128:(i + 1) * 128, :], d1.t[:])], [d1], [hS['dtb']])
        sti[0] += 1
        P.dma('pool', 'sto%d' % (sti[0] % 4), [(S['adtb'][i * 128:(i + 1) * 128, :], a1.t[:])], [a1], [hS['adtb']])
        sti[0] += 1
    P.end()


def load_cols(k, dst_ap, src_ap, r, ps, wr):
    P = k.P
    raw = P.sbuf([16, 128], F32, 'raw')
    P.dma('sp', 'ldc', [(raw.t[0:r, :], src_ap)], (), [raw])
    TR(P, ps.t[:, 0:r], raw.t[0:r, :], k.c['ident'].t[0:r, 0:r], [raw, k.c['ident']], [ps])
    CP(P, dst_ap, ps.t[:, 0:r], [ps], wr)


def phase_pool(k, l, s):
    P, I, S, PS, c, hS = k.P, k.I, k.S, k.PS, k.c, k.hS
    P.begin()
    segs = [(0, NL, 'rcl')] + ([(NL, NCX, 'rcc')] if l == 0 else [])
    pw = P.sbuf([128, 4, 128], BF16, 'pw')
    P.dma('pool', 'ld2', [(pw.t[:, g, :], I['pool_w'][l, g]) for g in range(4)], (), [pw])
    psc = P.sbuf([128, 4], F32, 'psc')
    load_cols(k, psc.t[:], I['pool_scale'][l].rearrange("(g p) -> g p", p=128), 4, PS[7], [psc])
    W = NL + 32
    sets = {e: [P.sbuf([128, W], F32, 'pb%s%d' % (e, i)) for i in range(3)] for e in ('dve',)}
    rcs = {e: P.sbuf([128, NL], F32, 'rc' + e) for e in ('dve',)}
    pls = {e: P.sbuf([128, NL], BF16, 'pl' + e) for e in ('dve',)}
    srot = P.rot(3, [128, 512], BF16, 'stg')
    offs = [(-1, 0), (-1, 1), (-2, 2), (-4, 4)]
    si = 0
    for (t0, n, rcn) in segs:
        Wn = n + 32
        rng = [(1, Wn), (2, Wn - 1), (4, Wn - 3), (8, Wn - 7)]
        for g in range(4):
            e = 'dve'
            xa, xb, xc = sets[e]
            rc, pl = rcs[e], pls[e]
            MEMSET(P, xa.t[:, 0:Wn], 0.0, [xa], eng=e)
            P.dma('pool', 'ld3', [(xa.t[:, 16:16 + n], S['poolT'][g * 128:(g + 1) * 128, t0:t0 + n])], [hS['poolT']], [xa])
            P.dma('sp', 'ld4', [(rc.t[:, 0:n], k.C[rcn][g].partition_broadcast(128))], (), [rc])
            srcs = [xa, xb, xc, xb]
            dsts = [xb, xc, xb, xc]
            for lv in range(g + 1):
                a, b = offs[lv]
                lo, hi = rng[lv]
                sr, ds = srcs[lv], dsts[lv]
                TT(P, ds.t[:, lo:hi], sr.t[:, lo + a:hi + a], sr.t[:, lo + b:hi + b], ALU.add, [sr], [ds], eng=e)
            fin = dsts[g]
            TT(P, fin.t[:, 16:16 + n], fin.t[:, 16:16 + n], rc.t[:, 0:n], ALU.mult, [fin, rc], [fin], eng=e)
            TT(P, pl.t[:, 0:n], fin.t[:, 16:16 + n], xa.t[:, 16:16 + n], ALU.subtract, [fin, xa], [pl], eng=e)
            for (b0, m) in tokblocks(0, n):
                pr = PS[si % 4]
                MM(P, pr.t[:, 0:m], pw.t[:, g, :], pl.t[:, b0:b0 + m], True, True, [pw, pl], [pr])
                st = srot.next()
                ACT(P, st.t[:, 0:m], pr.t[:, 0:m], AF.Copy, [pr, psc], [st], scale=psc.t[:, g:g + 1])
                P.dma('pool', 'sto%d' % (si % 4), [(S['ybrT'][0, g * 128:(g + 1) * 128, t0 + b0:t0 + b0 + m], st.t[:, 0:m])], [st], [hS['ybrT']])
                si += 1
    P.end()


def phase_ssd(k, l, s):
    P, I, S, PS, c, hS = k.P, k.I, k.S, k.PS, k.c, k.hS
    P.begin()
    raw = P.sbuf([6, 768], F32, 'cwraw')
    P.dma('sp', 'ld0', [(raw.t[0:5, :], I['conv_w'][l])], (), [raw])
    P.dma('sp', 'ld1', [(raw.t[5:6, :], I['conv_b'][l:l + 1, :])], (), [raw])
    cwb = P.sbuf([128, 6, 6], F32, 'cwb')
    for ch in range(6):
        TR(P, PS[0].t[:, ch * 8:ch * 8 + 6], raw.t[:, ch * 128:(ch + 1) * 128], c['ident'].t[0:6, 0:6], [raw, c['ident']], [PS[0]])
    for ch in range(6):
        CP(P, cwb.t[:, ch, :], PS[0].t[:, ch * 8:ch * 8 + 6], [PS[0]], [cwb])
    xrot = P.rot(2, [128, NL + 4], F32, 'cx')
    arot = P.rot(2, [128, NL], F32, 'cacc')
    orot = P.rot(2, [128, NL], BF16, 'cact')
    trot = P.rot(2, [128, 4, 128], BF16, 'ctk')
    psb = [PS[i].t[:].bitcast(BF16) for i in range(8)]
    si = 0
    ti = 0
    for (t0, n) in ((NL, NCX), (0, NL)):
        for ch in range(6):
            xin = xrot.next()
            MEMSET(P, xin.t[:, 0:2], 0.0, [xin])
            MEMSET(P, xin.t[:, 2 + n:4 + n], 0.0, [xin])
            P.dma('pool', 'ld3', [(xin.t[:, 2:2 + n], S['xbcT'][ch * 128:(ch + 1) * 128, t0:t0 + n])], [hS['xbcT']], [xin])
            acc = arot.next()
            TS(P, acc.t[:, 0:n], xin.t[:, 0:n], cwb.t[:, ch, 0:1], ALU.mult, [xin, cwb], [acc])
            for kk in range(1, 5):
                STT(P, acc.t[:, 0:n], xin.t[:, kk:kk + n], cwb.t[:, ch, kk:kk + 1], acc.t[:, 0:n], ALU.mult, ALU.add, [xin, cwb, acc], [acc])
            xa = orot.next()
            ACT(P, xa.t[:, 0:n], acc.t[:, 0:n], AF.Silu, [acc, cwb], [xa], bias=cwb.t[:, ch, 5:6])
            P.dma('pool', 'sto%d' % (si % 4), [(S['xactT'][ch * 128:(ch + 1) * 128, t0:t0 + n], xa.t[:, 0:n])], [xa], [hS['xactT']])
            si += 1
            if ch < 5:
                nt = n // 128
                for jb in range(0, nt, 4):
                    jn = min(4, nt - jb)
                    pb = PS[4 + ti % 4]
                    pbb = psb[4 + ti % 4]
                    ti += 1
                    for j in range(jn):
                        TR(P, pbb[:, j * 128:(j + 1) * 128], xa.t[:, (jb + j) * 128:(jb + j + 1) * 128], c['identb'].t[:], [xa, c['identb']], [pb])
                    tk = trot.next()
                    CP(P, tk.t[:, 0:jn, :], pbb[:, 0:jn * 128].rearrange("p (j f) -> p j f", f=128), [pb], [tk], eng='act')
                    P.dma('pool', 'sto%d' % (si % 4), [(S['xtok'][t0 + jb * 128:t0 + (jb + jn) * 128, ch * 128:(ch + 1) * 128].rearrange("(j p) f -> p j f", p=128),
                                                      tk.t[:, 0:jn, :])], [tk], [hS['xtok']])
                    si += 1
    dsk = P.sbuf([128, 8], F32, 'dsk')
    P.dma('sp', 'ld0', [(dsk.t[:], I['ssd_d'][l].partition_broadcast(128))], (), [dsk])
    grow = P.sbuf([128, 512], F32, 'grow')
    P.dma('sp', 'ld1', [(grow.t[:], I['ssd_norm_g'][l].partition_broadcast(128))], (), [grow])
    st = P.sbuf([128, 256], F32, 'st')
    stbf = P.sbuf([128, 256], BF16, 'stbf')
    adr = P.rot(2, [128, 16], F32, 'adt')
    dtr = P.rot(2, [128, 16], F32, 'dt')
    xtr = P.rot(2, [128, 640], BF16, 'xtk')
    bzr = P.rot(2, [128, 2, 128], BF16, 'bz')
    czr = P.rot(2, [128, 2, 128], BF16, 'cz')
    for tl in bzr.tiles + czr.tiles:
        MEMSET(P, tl.t[:], 0.0, [tl])
    ztr = P.rot(2, [128, 512], BF16, 'zt')
    yfr = P.rot(2, [128, 512], F32, 'yft')
    acr = P.rot(2, [128, 24], F32, 'acs')
    exr = P.rot(2, [128, 24], F32, 'ex')
    wdr = P.rot(2, [128, 8], F32, 'wd')
    xdr = P.rot(2, [128, 8, 64], BF16, 'xdt')
    xddr = P.rot(2, [128, 8, 64], BF16, 'xdtd')
    lhr = P.rot(3, [128, 128], F32, 'lh')
    Lr = P.rot(2, [128, 2, 512], F32, 'L')
    LSr = P.rot(2, [128, 8, 128], BF16, 'LS')
    ydr = P.rot(2, [128, 512], F32, 'yd')
    ycr = P.rot(2, [128, 8, 64], F32, 'yc')
    tmr = P.rot(2, [128, 8, 64], F32, 'tmp')
    zsr = P.rot(2, [128, 512], F32, 'zs')
    ssr = P.rot(2, [128, 4], F32, 'ss')
    ynr = P.rot(2, [128, 512], BF16, 'yn')
    ytr = P.rot(2, [128, 4, 128], BF16, 'ytk')
    for d in range(2):
        MEMSET(P, st.t[:], 0.0, [st])
        MEMSET(P, stbf.t[:], 0.0, [stbf])
        chunks = [16, 17] + list(range(16)) if d == 0 else [17, 16] + list(range(15, -1, -1))
        tm_d, g_d, neg_d = (c['tm'], c['g'], c['negf']) if d == 0 else (c['tmT'], c['gT'], c['negb'])
        hd = d * 8
        for cch in chunks:
            with_out = (cch < 16) or (l == 0)
            r0 = cch * 128
            adt, dt, xt, bz, cz = adr.next(), dtr.next(), xtr.next(), bzr.next(), czr.next()
            P.dma('sp', 'la', [(adt.t[:], S['adtb'][r0:r0 + 128, :])], [hS['adtb']], [adt])
            P.dma('sp', 'lb', [(dt.t[:], S['dtb'][r0:r0 + 128, :])], [hS['dtb']], [dt])
            P.dma('sp', 'lc', [(xt.t[:], S['xtok'][r0:r0 + 128, :])], [hS['xtok']], [xt])
            P.dma('sp', 'ld', [(bz.t[gi * 64:(gi + 1) * 64, gi, :], S['xactT'][512 + gi * 64:512 + (gi + 1) * 64, r0:r0 + 128]) for gi in range(2)], [hS['xactT']], [bz])
            P.dma('sp', 'lg', [(cz.t[gi * 64:(gi + 1) * 64, gi, :], S['xactT'][640 + gi * 64:640 + (gi + 1) * 64, r0:r0 + 128]) for gi in range(2)], [hS['xactT']], [cz])
            MM(P, PS[0].t[:, 0:8], tm_d.t[:], adt.t[:, hd:hd + 8], True, True, [tm_d, adt], [PS[0]])
            MM(P, PS[0].t[:, 8:16], c['ones'].t[:], adt.t[:, hd:hd + 8], True, True, [c['ones'], adt], [PS[0]])
            acs, ex = acr.next(), exr.next()
            CP(P, acs.t[:, 0:16], PS[0].t[:, 0:16], [PS[0]], [acs])
            TT(P, acs.t[:, 16:24], acs.t[:, 8:16], acs.t[:, 0:8], ALU.subtract, [acs], [acs])
            ACT(P, ex.t[:], acs.t[:], AF.Exp, [acs], [ex])
            wd = wdr.next()
            TT(P, wd.t[:], dt.t[:, hd:hd + 8], ex.t[:, 16:24], ALU.mult, [dt, ex], [wd])
            xs3 = xt.t[:, 0:512].rearrange("p (h e) -> p h e", e=64)
            xdt, xdtd = xdr.next(), xddr.next()
            TT(P, xdt.t[:], xs3, dt.t[:, hd:hd + 8].unsqueeze(2).to_broadcast([128, 8, 64]), ALU.mult, [xt, dt], [xdt])
            TT(P, xdtd.t[:], xs3, wd.t[:].unsqueeze(2).to_broadcast([128, 8, 64]), ALU.mult, [xt, wd], [xdtd])
            if with_out:
                for gi in range(2):
                    MM(P, PS[1].t[:, gi * 128:(gi + 1) * 128], bz.t[:, gi, :], cz.t[:, gi, :], True, True, [bz, cz], [PS[1]])
                for h in range(8):
                    lh = lhr.next()
                    TS(P, lh.t[:], g_d.t[:], adt.t[:, hd + h:hd + h + 1], ALU.mult, [g_d, adt], [lh])
                    pg = PS[2 + h // 4]
                    MM(P, pg.t[:, (h % 4) * 128:(h % 4 + 1) * 128], lh.t[:], tm_d.t[:], True, False, [lh, tm_d], [pg])
                    MM(P, pg.t[:, (h % 4) * 128:(h % 4 + 1) * 128], c['ident'].t[:], neg_d.t[:], False, True, [c['ident'], neg_d], [pg])
                L, LS = Lr.next(), LSr.next()
                for gi in range(2):
                    ACT(P, L.t[:, gi, :], PS[2 + gi].t[:], AF.Exp, [PS[2 + gi]], [L])
                for gi in range(2):
                    TT(P, LS.t[:, gi * 4:(gi + 1) * 4, :], L.t[:, gi, :].rearrange("p (h e) -> p h e", e=128),
                       PS[1].t[:, gi * 128:(gi + 1) * 128].unsqueeze(1).to_broadcast([128, 4, 128]), ALU.mult, [L, PS[1]], [LS])
                for h in range(8):
                    MM(P, PS[4].t[:, h * 64:(h + 1) * 64], LS.t[:, h, :], xdt.t[:, h, :], True, True, [LS, xdt], [PS[4]])
                for gi in range(2):
                    MM(P, PS[5].t[:, gi * 256:(gi + 1) * 256], cz.t[:, gi, :], stbf.t[:, :], True, True, [cz, stbf], [PS[5]])
                yd, yc = ydr.next(), ycr.next()
                CP(P, yd.t[:], PS[4].t[:], [PS[4]], [yd], eng='act')
                TT(P, yc.t[:], PS[5].t[:].rearrange("p (h e) -> p h e", e=64), ex.t[:, 0:8].unsqueeze(2).to_broadcast([128, 8, 64]), ALU.mult, [PS[5], ex], [yc])
                ycf = yc.t[:].rearrange("p h e -> p (h e)")
                TT(P, ycf, ycf, yd.t[:], ALU.add, [yc, yd], [yc])
            MM(P, PS[6].t[:], xt.t[:, 512:640], xdtd.t[:].rearrange("p h e -> p (h e)"), True, True, [xt, xdtd], [PS[6]])
            for gi in range(2):
                rs = slice(gi * 64, (gi + 1) * 64)
                st3 = st.t[rs, :].rearrange("p (h e) -> p h e", e=64)
                TT(P, st3, st3, ex.t[rs, 8 + gi * 4:12 + gi * 4].unsqueeze(2).to_broadcast([64, 4, 64]), ALU.mult, [st, ex], [st])
                TT(P, st.t[rs, :], st.t[rs, :], PS[6].t[rs, gi * 256:(gi + 1) * 256], ALU.add, [st, PS[6]], [st])
            CP(P, stbf.t[:], st.t[:], [st], [stbf], eng='act')
            if not with_out:
                continue
            if d == 0:
                P.dma('pool', 'sto%d' % (si % 4), [(S['yf'][r0:r0 + 128, :], ycf)], [yc], [hS['yf']])
                si += 1
                continue
            zt, yft = ztr.next(), yfr.next()
            P.dma('sp', 'le', [(zt.t[:], S['zt'][r0:r0 + 128, :])], [hS['zt']], [zt])
            P.dma('sp', 'lf', [(yft.t[:], S['yf'][r0:r0 + 128, :])], [hS['yf']], [yft])
            TT(P, ycf, ycf, yft.t[:], ALU.add, [yc, yft], [yc])
            tmp = tmr.next()
            TT(P, tmp.t[:], xs3, dsk.t[:].unsqueeze(2).to_broadcast([128, 8, 64]), ALU.mult, [xt, dsk], [tmp])
            TT(P, yc.t[:], yc.t[:], tmp.t[:], ALU.add, [yc, tmp], [yc])
            zs = zsr.next()
            ACT(P, zs.t[:], zt.t[:], AF.Silu, [zt], [zs])
            TT(P, ycf, ycf, zs.t[:], ALU.mult, [yc, zs], [yc])
            ss = ssr.next()
            ACT(P, zs.t[:], ycf, AF.Square, [yc], [zs, ss], accum=ss.t[:, 0:1])
            TS(P, ss.t[:, 1:2], ss.t[:, 0:1], 1.0 / 512.0, ALU.mult, [ss], [ss], s2=LN_EPS, op1=ALU.add)
            ACT(P, ss.t[:, 1:2], ss.t[:, 1:2], AF.Sqrt, [ss], [ss])
            RECIP(P, ss.t[:, 2:3], ss.t[:, 1:2], [ss], [ss])
            yn = ynr.next()
            STT(P, yn.t[:], ycf, ss.t[:, 2:3], grow.t[:], ALU.mult, ALU.mult, [yc, ss, grow], [yn])
            for j in range(4):
                TR(P, psb[7][:, j * 128:(j + 1) * 128], yn.t[:, j * 128:(j + 1) * 128], c['identb'].t[:], [yn, c['identb']], [PS[7]])
            ytk = ytr.next()
            CP(P, ytk.t[:], psb[7][:, 0:512].rearrange("p (j f) -> p j f", f=128), [PS[7]], [ytk], eng='act')
            P.dma('pool', 'sto%d' % (si % 4), [(S['ybrT'][1, :, r0:r0 + 128].rearrange("(j p) t -> p j t", p=128), ytk.t[:])], [ytk], [hS['ybrT']])
            si += 1
    P.end()


def phase_attn(k, l, s):
    P, I, S, PS, c, hS = k.P, k.I, k.S, k.PS, k.c, k.hS
    P.begin()
    lam_init = 0.8 - 0.6 * math.exp(-0.3 * l)
    qT = P.sbuf([128, 4, NT], BF16, 'qT')
    kTz = [P.sbuf([128, 4, NT], BF16, 'kTz%d' % b) for b in range(2)]
    for b in range(2):
        MEMSET(P, kTz[b].t[:], 0.0, [kTz[b]])
    v = P.sbuf([128, NTILES, 512], BF16, 'v')
    P.dma('sp', 'ld0', [(qT.t[:], S['qT'].rearrange("(c p) t -> p c t", p=128))], [hS['qT']], [qT])
    for b in range(2):
        P.dma('sp', 'ld1', [(kTz[b].t[b * 64:(b + 1) * 64, :, :], S['kT'].rearrange("(c p) t -> p c t", p=128)[b * 64:(b + 1) * 64, :, :])], [hS['kT']], [kTz[b]])
    P.dma('sp', 'ld2', [(v.t[:], S['vtok'].rearrange("(j p) f -> p j f", p=128))], [hS['vtok']], [v])
    dl = P.sbuf([128, 4, 64], F32, 'dl')
    P.dma('sp', 'ld3', [(dl.t[:], I['diff_lambda'][l].partition_broadcast(128))], (), [dl])
    pr2 = P.sbuf([128, 2, 64], F32, 'pr2')
    TT(P, pr2.t[:, 0, :], dl.t[:, 0, :], dl.t[:, 1, :], ALU.mult, [dl], [pr2])
    TT(P, pr2.t[:, 1, :], dl.t[:, 2, :], dl.t[:, 3, :], ALU.mult, [dl], [pr2])
    lm = P.sbuf([128, 4], F32, 'lm')
    P.op('dve', lambda e: e.tensor_reduce(out=lm.t[:, 0:2], in_=pr2.t[:], axis=AX.X, op=ALU.add), [pr2], [lm])
    ACT(P, lm.t[:, 0:2], lm.t[:, 0:2], AF.Exp, [lm], [lm])
    TT(P, lm.t[:, 2:3], lm.t[:, 1:2], lm.t[:, 0:1], ALU.subtract, [lm], [lm])
    TS(P, lm.t[:, 3:4], lm.t[:, 2:3], -lam_init, ALU.add, [lm], [lm])
    gcol = P.sbuf([128, 2], F32, 'gcol')
    load_cols(k, gcol.t[:], I['diff_norm_g'][0:2, :], 2, PS[7], [gcol])
    gc = P.sbuf([128, 1], F32, 'gc')
    TS(P, gc.t[:], gcol.t[:, l:l + 1], 1.0 - lam_init, ALU.mult, [gcol], [gc])
    erot = P.rot(3, [128, 512], BF16, 'E')
    frot = [P.rot(2, [128, 512], F32, 'af%d' % i) for i in range(4)]
    srot = P.rot(2, [128, 512], BF16, 'stg')
    qblocks = [(q0, 512, list(range(NTILES))) for q0 in range(0, NL, 512)]
    if l == 0:
        qblocks.append((NL, NCX, [16, 17]))
    si = 0
    for hh in range(4):
        for (q0, nq, ktiles) in qblocks:
            for b in range(2):
                pb = slice(b * 64, (b + 1) * 64)
                for idx, kt in enumerate(ktiles):
                    psc = PS[idx % 2]
                    MM(P, psc.t[:, 0:nq], kTz[b].t[:, hh, kt * 128:(kt + 1) * 128], qT.t[:, hh, q0:q0 + nq], True, True, [kTz[b], qT], [psc])
                    E = erot.next()
                    ACT(P, E.t[:, 0:nq], psc.t[:, 0:nq], AF.Exp, [psc], [E], scale=0.125)
                    last = idx == len(ktiles) - 1
                    MM(P, PS[2 + b].t[:, 0:nq], v.t[:, kt, hh * 128:(hh + 1) * 128], E.t[:, 0:nq], idx == 0, last, [v, E], [PS[2 + b]])
                    MM(P, PS[4 + b].t[:, 0:nq], c['onesb'].t[:], E.t[:, 0:nq], idx == 0, last, [c['onesb'], E], [PS[4 + b]])
            r1, o1, r2, o2 = frot[0].next(), frot[1].next(), frot[2].next(), frot[3].next()
            RECIP(P, r1.t[:, 0:nq], PS[4].t[:, 0:nq], [PS[4]], [r1])
            TT(P, o1.t[:, 0:nq], PS[2].t[:, 0:nq], r1.t[:, 0:nq], ALU.mult, [PS[2], r1], [o1])
            RECIP(P, r2.t[:, 0:nq], PS[5].t[:, 0:nq], [PS[5]], [r2])
            TT(P, o2.t[:, 0:nq], PS[3].t[:, 0:nq], r2.t[:, 0:nq], ALU.mult, [PS[3], r2], [o2])
            STT(P, o1.t[:, 0:nq], o2.t[:, 0:nq], lm.t[:, 3:4], o1.t[:, 0:nq], ALU.mult, ALU.add, [o2, lm, o1], [o1])
            TT(P, r1.t[:, 0:nq], o1.t[:, 0:nq], o1.t[:, 0:nq], ALU.mult, [o1], [r1])
            MM(P, PS[6].t[:, 0:nq], c['ones'].t[:], r1.t[:, 0:nq], True, True, [c['ones'], r1], [PS[6]])
            ACT(P, r2.t[:, 0:nq], PS[6].t[:, 0:nq], AF.Sqrt, [PS[6]], [r2], scale=1.0 / 128.0, bias=LN_EPS)
            RECIP(P, r2.t[:, 0:nq], r2.t[:, 0:nq], [r2], [r2])
            st = srot.next()
            STT(P, st.t[:, 0:nq], o1.t[:, 0:nq], gc.t[:, 0:1], r2.t[:, 0:nq], ALU.mult, ALU.mult, [o1, gc, r2], [st])
            P.dma('pool', 'sto%d' % (si % 4), [(S['ybrT'][2, hh * 128:(hh + 1) * 128, q0:q0 + nq], st.t[:, 0:nq])], [st], [hS['ybrT']])
            si += 1
    P.end()


def phase_sgu(k, l, s):
    P, I, S, PS, c, hS = k.P, k.I, k.S, k.PS, k.c, k.hS
    P.begin()
    wsT = P.sbuf([128, 4, 128], BF16, 'wsT')
    wraw = P.sbuf([128, 4, 128], F32, 'wraw')
    P.dma('sp', 'ld0', [(wraw.t[:], I['sgu_w'][l].rearrange("g p q -> p g q"))], (), [wraw])
    for g in range(4):
        TR(P, PS[7].t[:, g * 128:(g + 1) * 128], wraw.t[:, g, :], c['ident'].t[:], [wraw, c['ident']], [PS[7]])
    CP(P, wsT.t[:], PS[7].t[:].rearrange("p (g q) -> p g q", q=128), [PS[7]], [wsT], eng='act')
    bsb = P.sbuf([128, 4, 128], F32, 'bsb')
    P.dma('sp', 'ld1', [(bsb.t[:], I['sgu_b'][l].partition_broadcast(128))], (), [bsb])
    svr = P.rot(3, [128, 512], BF16, 'sv')
    utr = P.rot(3, [128, 4, 128], BF16, 'ut')
    tr_ = P.rot(2, [128, 512], F32, 't')
    str_ = P.rot(3, [128, 4, 128], BF16, 'stg')
    nch = NTILES if l == 0 else 16
    for cch in range(nch):
        r0 = cch * 128
        sv, ut = svr.next(), utr.next()
        P.dma('sp', 'la%d' % (cch % 2), [(sv.t[:], S['svn'][r0:r0 + 128, :])], [hS['svn']], [sv])
        P.dma('sp', 'lb%d' % (cch % 2), [(ut.t[:], S['uT'][:, r0:r0 + 128].rearrange("(g d) t -> d g t", d=128))], [hS['uT']], [ut])
        pr = PS[cch % 4]
        for g in range(4):
            MM(P, pr.t[:, g * 128:(g + 1) * 128], sv.t[:, g * 128:(g + 1) * 128], wsT.t[:, g, :], True, True, [sv, wsT], [pr])
        t = tr_.next()
        TT(P, t.t[:], pr.t[:], bsb.t[:].rearrange("p g q -> p (g q)"), ALU.add, [pr, bsb], [t])
        st = str_.next()
        TT(P, st.t[:].rearrange("p g q -> p (g q)"), t.t[:], ut.t[:].rearrange("p g q -> p (g q)"), ALU.mult, [t, ut], [st])
        P.dma('pool', 'sto%d' % (cch % 4), [(S['ybrT'][3, :, r0:r0 + 128].rearrange("(g d) t -> d g t", d=128), st.t[:])], [st], [hS['ybrT']])
    P.end()


def xsrc(k, l, s, i):
    if l == 0:
        return (k.I['x'][s, i * 128:(i + 1) * 128, :] if i < 16 else k.I['ctx'][s, (i - 16) * 128:(i - 15) * 128, :]), []
    return k.S['xc'][s, i * 128:(i + 1) * 128, :], [k.hS['xc']]


def layer_norm_tile(P, t, n2, st6r, mvr, grow, brow, out_ap, wr):
    s6, m2 = st6r.next(), mvr.next()
    for j in range(2):
        P.op('dve', lambda e, j=j: e.bn_stats(out=s6.t[:, j * 6:(j + 1) * 6], in_=t.t[:, j * 512:(j + 1) * 512]), [t], [s6])
    P.op('dve', lambda e: e.bn_aggr(out=m2.t[:, 0:2], in_=s6.t[:, 0:12]), [s6], [m2])
    TS(P, m2.t[:, 2:3], m2.t[:, 1:2], LN_EPS, ALU.add, [m2], [m2])
    ACT(P, m2.t[:, 2:3], m2.t[:, 2:3], AF.Sqrt, [m2], [m2])
    RECIP(P, m2.t[:, 3:4], m2.t[:, 2:3], [m2], [m2])
    TS(P, t.t[:], t.t[:], m2.t[:, 0:1], ALU.subtract, [t, m2], [t], s2=m2.t[:, 3:4], op1=ALU.mult)
    TT(P, t.t[:], t.t[:], grow.t[:], ALU.mult, [t, grow], [t])
    TT(P, out_ap, t.t[:], brow.t[:], ALU.add, [t, brow], wr)


def phase_merge(k, l, s):
    P, I, S, PS, c, hS = k.P, k.I, k.S, k.PS, k.c, k.hS
    P.begin()
    wg = P.sbuf([128, 4, 8, D], BF16, 'wg')
    wb = P.sbuf([128, 4, 4, D], BF16, 'wb')
    for br in range(4):
        P.dma('pool', 'wg%d' % (br % 2), [(wg.t[:, br, :, :], I['w_gate'][l, br].rearrange("(kc p) n -> p kc n", p=128))], (), [wg])
        P.dma('pool', 'wb%d' % (br % 2), [(wb.t[:, br, :, :], I['w_branch'][l, br].rearrange("(kc p) n -> p kc n", p=128))], (), [wb])
    hr = P.rot(2, [128, 8, 512], BF16, 'hblk')
    yr = P.rot(2, [128, 4, 4, 512], BF16, 'yblk')
    sgr = P.rot(2, [128, 512], F32, 'sg')
    tr_ = P.rot(2, [128, 512], F32, 'tt')
    ar = P.rot(2, [128, 512], F32, 'acc')
    srot = P.rot(3, [128, 512], BF16, 'stg')
    blocks = tokblocks(0, NL) + (tokblocks(NL, NCX) if l == 0 else [])
    cnt = 0
    si = 0
    for bi, (t0, n) in enumerate(blocks):
        hb, yb = hr.next(), yr.next()
        P.dma('sp', 'la%d' % (bi % 2), [(hb.t[:, :, 0:n], S['hTd'][:, t0:t0 + n].rearrange("(kc p) t -> p kc t", p=128))], [hS['hTd']], [hb])
        P.dma('sp', 'lb%d' % (bi % 2), [(yb.t[:, br, :, 0:n], S['ybrT'][br, :, t0:t0 + n].rearrange("(kc p) t -> p kc t", p=128)) for br in range(4)], [hS['ybrT']], [yb])
        for dc in range(8):
            acc = ar.next()
            for br in range(4):
                pg, pp = PS[cnt % 2], PS[2 + cnt % 2]
                cnt += 1
                for kc in range(8):
                    MM(P, pg.t[:, 0:n], wg.t[:, br, kc, dc * 128:(dc + 1) * 128], hb.t[:, kc, 0:n], kc == 0, kc == 7, [wg, hb], [pg])
                for kc in range(4):
                    MM(P, pp.t[:, 0:n], wb.t[:, br, kc, dc * 128:(dc + 1) * 128], yb.t[:, br, kc, 0:n], kc == 0, kc == 3, [wb, yb], [pp])
                sg = sgr.next()
                ACT(P, sg.t[:, 0:n], pg.t[:, 0:n], AF.Sigmoid, [pg], [sg])
                if br == 0:
                    TT(P, acc.t[:, 0:n], sg.t[:, 0:n], pp.t[:, 0:n], ALU.mult, [sg, pp], [acc])
                else:
                    t = tr_.next()
                    TT(P, t.t[:, 0:n], sg.t[:, 0:n], pp.t[:, 0:n], ALU.mult, [sg, pp], [t])
                    if br < 3:
                        TT(P, acc.t[:, 0:n], acc.t[:, 0:n], t.t[:, 0:n], ALU.add, [acc, t], [acc])
                    else:
                        st = srot.next()
                        TT(P, st.t[:, 0:n], acc.t[:, 0:n], t.t[:, 0:n], ALU.add, [acc, t], [st])
                        P.dma('pool', 'sto%d' % (si % 4), [(S['accT'][dc * 128:(dc + 1) * 128, t0:t0 + n], st.t[:, 0:n])], [st], [hS['accT']])
                        si += 1
    P.end()


def load_row(k, tile, src_ap, key, rd=()):
    k.P.dma('sp', key, [(tile.t[:], src_ap.partition_broadcast(128))], list(rd), [tile])


def phase_out1(k, l, s):
    P, I, S, PS, c, hS = k.P, k.I, k.S, k.PS, k.c, k.hS
    P.begin()
    wo = P.sbuf([128, 8, D], BF16, 'wo')
    P.dma('pool', 'wg0', [(wo.t[:], I['w_out'][l].rearrange("(kc p) n -> p kc n", p=128))], (), [wo])
    rows = {}
    for which, r in (('lat', s), ('ctx', 2)):
        for nm, a in (('m2', 2 * D), ('m3', 3 * D), ('m4', 4 * D)):
            t = P.sbuf([128, D], F32, 'row')
            load_row(k, t, S['modrow'][l, r, a:a + D], 'ld%d' % (len(rows) % 4), [hS['modrow']])
            rows[(nm, which)] = t
        TS(P, rows[('m4', which)].t[:], rows[('m4', which)].t[:], 1.0, ALU.add, [rows[('m4', which)]], [rows[('m4', which)]])
    g1 = P.sbuf([128, D], F32, 'g1')
    b1 = P.sbuf([128, D], F32, 'b1')
    load_row(k, g1, I['ln1_g'][l], 'ld0')
    load_row(k, b1, I['ln1_b'][l], 'ld1')
    wr = P.sbuf([128, 8, 16], F32, 'wr')
    P.dma('sp', 'ld2', [(wr.t[:], I['w_router'][l].rearrange("(kc p) e -> p kc e", p=128))], (), [wr])
    atr = P.rot(2, [128, 8, 128], BF16, 'at')
    xtr = P.rot(2, [128, D], F32, 'xt')
    ttr = P.rot(2, [128, D], F32, 'tt')
    x1r = P.rot(2, [128, D], F32, 'x1')
    hfr = P.rot(2, [128, D], F32, 'hf')
    hbr = P.rot(2, [128, D], BF16, 'hfb')
    hTr = P.rot(2, [128, 8, 128], F32, 'hfT')
    st6r = P.rot(2, [128, 12], F32, 's6')
    mvr = P.rot(2, [128, 4], F32, 'mv')
    smr = P.rot(2, [128, 4], F32, 'sm')
    exr = P.rot(2, [128, 16], F32, 'ex')
    tiles = NTILES if l == 0 else 16
    si = 0
    for i in range(tiles):
        which = 'lat' if i < 16 else 'ctx'
        at, xt = atr.next(), xtr.next()
        P.dma('sp', 'la%d' % (i % 2), [(at.t[:], S['accT'][:, i * 128:(i + 1) * 128].rearrange("(kc p) t -> p kc t", p=128))], [hS['accT']], [at])
        src, rd = xsrc(k, l, s, i)
        P.dma('sp', 'lb%d' % (i % 2), [(xt.t[:], src)], rd, [xt])
        for j in range(2):
            for kc in range(8):
                MM(P, PS[j].t[:], at.t[:, kc, :], wo.t[:, kc, j * 512:(j + 1) * 512], kc == 0, kc == 7, [at, wo], [PS[j]])
        t = ttr.next()
        m2 = rows[('m2', which)]
        for j in range(2):
            TT(P, t.t[:, j * 512:(j + 1) * 512], PS[j].t[:], m2.t[:, j * 512:(j + 1) * 512], ALU.mult, [PS[j], m2], [t])
        STT(P, t.t[:], xt.t[:], ALPHA, t.t[:], ALU.mult, ALU.add, [xt, t], [t])
        x1 = x1r.next()
        layer_norm_tile(P, t, None, st6r, mvr, g1, b1, x1.t[:], [x1])
        P.dma('pool', 'sto%d' % (si % 4), [(S['xm'][i * 128:(i + 1) * 128, :], x1.t[:])], [x1], [hS['xm']])
        si += 1
        hf, hb = hfr.next(), hbr.next()
        TT(P, hf.t[:], x1.t[:], rows[('m4', which)].t[:], ALU.mult, [x1, rows[('m4', which)]], [hf])
        TT(P, hf.t[:], hf.t[:], rows[('m3', which)].t[:], ALU.add, [hf, rows[('m3', which)]], [hf])
        CP(P, hb.t[:], hf.t[:], [hf], [hb], eng='act')
        P.dma('pool', 'sto%d' % (si % 4), [(S['hftok'][i * 128:(i + 1) * 128, :], hb.t[:])], [hb], [hS['hftok']])
        si += 1
        hfT = hTr.next()
        for kc in range(8):
            TR(P, PS[2 + kc // 4].t[:, (kc % 4) * 128:(kc % 4 + 1) * 128], hf.t[:, kc * 128:(kc + 1) * 128], c['ident'].t[:], [hf, c['ident']], [PS[2 + kc // 4]])
        for j in range(2):
            CP(P, hfT.t[:, j * 4:(j + 1) * 4, :], PS[2 + j].t[:].rearrange("p (a f) -> p a f", f=128), [PS[2 + j]], [hfT], eng='act')
        for kc in range(8):
            MM(P, PS[4].t[:, 0:16], hfT.t[:, kc, :], wr.t[:, kc, :], kc == 0, kc == 7, [hfT, wr], [PS[4]])
        sm, ex = smr.next(), exr.next()
        P.op('dve', lambda e, sm=sm: e.tensor_reduce(out=sm.t[:, 0:1], in_=PS[4].t[:, 0:16], axis=AX.X, op=ALU.max), [PS[4]], [sm])
        TS(P, sm.t[:, 1:2], sm.t[:, 0:1], -1.0, ALU.mult, [sm], [sm])
        ACT(P, ex.t[:], PS[4].t[:, 0:16], AF.Exp, [PS[4], sm], [ex, sm], bias=sm.t[:, 1:2], accum=sm.t[:, 2:3])
        RECIP(P, sm.t[:, 3:4], sm.t[:, 2:3], [sm], [sm])
        TS(P, ex.t[:], ex.t[:], sm.t[:, 3:4], ALU.mult, [ex, sm], [ex])
        TR(P, PS[5].t[0:16, 0:128], ex.t[:], c['ident'].t[:], [ex, c['ident']], [PS[5]])
        CP(P, k.affT.t[:, i * 128:(i + 1) * 128], PS[5].t[0:16, 0:128], [PS[5]], [k.affT])
    P.end()


def phase_route(k, l, s):
    P, I, S, PS, c, hS = k.P, k.I, k.S, k.PS, k.c, k.hS
    P.begin()
    segs = [(0, NL, 256)] + ([(NL, NCX, 32)] if l == 0 else [])
    work = P.sbuf([16, NL], F32, 'work')
    maskT = P.sbuf([16, NL], F32, 'maskT')
    cs = P.sbuf([16, NL], F32, 'cs')
    gateT = P.sbuf([16, NL], F32, 'gateT')
    ones16 = P.sbuf([16, NL], F32, 'ones16')
    mx8 = P.sbuf([16, 8], F32, 'mx8')
    MEMSET(P, ones16.t[:], 1.0, [ones16])
    for (t0, n, cap) in segs:
        nt = n // 128
        aff = k.affT.t[:, t0:t0 + n]
        CP(P, work.t[:, 0:n], aff, [k.affT], [work])
        rounds = cap // 8
        for r in range(rounds):
            P.op('dve', lambda e, n=n: e.max(out=mx8.t[:], in_=work.t[:, 0:n]), [work], [mx8])
            if r < rounds - 1:
                P.op('dve', lambda e, n=n: e.match_replace(out=work.t[:, 0:n], in_to_replace=mx8.t[:], in_values=work.t[:, 0:n], imm_value=-1e30), [work, mx8], [work])
        TS(P, maskT.t[:, 0:n], aff, mx8.t[:, 7:8], ALU.is_ge, [k.affT, mx8], [maskT])
        P.op('dve', lambda e, n=n: e.tensor_tensor_scan(out=cs.t[:, 0:n], data0=ones16.t[:, 0:n], data1=maskT.t[:, 0:n], initial=0.0, op0=ALU.mult, op1=ALU.add),
             [ones16, maskT], [cs])
        TT(P, cs.t[:, 0:n], cs.t[:, 0:n], maskT.t[:, 0:n], ALU.mult, [cs, maskT], [cs])
        TS(P, k.posmT.t[:, t0:t0 + n], cs.t[:, 0:n], -1.0, ALU.add, [cs], [k.posmT])
        TT(P, gateT.t[:, 0:n], aff, maskT.t[:, 0:n], ALU.mult, [k.affT, maskT], [gateT])
        for j in range(nt):
            TR(P, PS[0].t[:, j * 16:(j + 1) * 16], k.posmT.t[:, t0 + j * 128:t0 + (j + 1) * 128], c['ident'].t[0:16, 0:16], [k.posmT, c['ident']], [PS[0]])
            TR(P, PS[1].t[:, j * 16:(j + 1) * 16], gateT.t[:, j * 128:(j + 1) * 128], c['ident'].t[0:16, 0:16], [gateT, c['ident']], [PS[1]])
        j0 = t0 // 128
        CP(P, k.posk.t[:, j0:j0 + nt, :], PS[0].t[:, 0:nt * 16].rearrange("p (j e) -> p j e", e=16), [PS[0]], [k.posk])
        CP(P, k.gatek.t[:, j0:j0 + nt, :], PS[1].t[:, 0:nt * 16].rearrange("p (j e) -> p j e", e=16), [PS[1]], [k.gatek], eng='act')
    P.end()


def phase_moe(k, l, s):
    phase_route(k, l, s)
    P, I, S, PS, c, hS = k.P, k.I, k.S, k.PS, k.c, k.hS
    P.begin()
    segs = [(0, NL, 256)] + ([(NL, NCX, 32)] if l == 0 else [])
    hft = P.sbuf([128, NTILES, D], BF16, 'hft')
    P.dma('sp', 'ld0', [(hft.t[:], S['hftok'].rearrange("(j p) d -> p j d", p=128))], [hS['hftok']], [hft])
    oacc = P.sbuf([128, 16, D], F32, 'oacc')
    w1r = P.rot(1, [128, 8, 512], BF16, 'w1')
    w3r = P.rot(1, [128, 8, 512], BF16, 'w3')
    w2r = P.rot(1, [128, 4, D], BF16, 'w2')
    selr = P.rot(1, [128, 16, 256], BF16, 'sel')
    seltr = P.rot(1, [128, 2, NL], BF16, 'selT')
    xer = P.rot(1, [128, 8, 256], BF16, 'xeT')
    hidr = P.rot(1, [128, 4, 256], BF16, 'hid')
    yer = P.rot(1, [128, 2, D], BF16, 'ye')
    s1r = P.rot(2, [128, 256], F32, 's1')
    si = 0
    scat = [PS[7], PS[0], PS[1]]
    sci = 0
    for (t0, n, cap) in segs:
        nt = n // 128
        j0 = t0 // 128
        nsc = max(1, cap // 128)
        scw = min(cap, 128)
        for e in range(16):
            w1e, w3e, w2e = w1r.next(), w3r.next(), w2r.next()
            P.dma('pool', 'w1%d' % (e % 2), [(w1e.t[:], I['w1'][l, e].rearrange("(kc p) f -> p kc f", p=128))], (), [w1e])
            P.dma('pool', 'w3%d' % (e % 2), [(w3e.t[:], I['w3'][l, e].rearrange("(kc p) f -> p kc f", p=128))], (), [w3e])
            P.dma('pool', 'w2%d' % (e % 2), [(w2e.t[:], I['w2'][l, e].rearrange("(kc p) f -> p kc f", p=128))], (), [w2e])
            sel, selT, xeT, hid, ye = selr.next(), seltr.next(), xer.next(), hidr.next(), yer.next()
            for j in range(nt):
                TS(P, sel.t[:, j, 0:cap], c['iota'].t[:, 0:cap], k.posk.t[:, j0 + j, e:e + 1], ALU.is_equal, [c['iota'], k.posk], [sel])
            for dc in range(8):
                pr = PS[2 + dc % 2]
                for j in range(nt):
                    MM(P, pr.t[:, 0:cap], hft.t[:, j0 + j, dc * 128:(dc + 1) * 128], sel.t[:, j, 0:cap], j == 0, j == nt - 1, [hft, sel], [pr])
                CP(P, xeT.t[:, dc, 0:cap], pr.t[:, 0:cap], [pr], [xeT], eng='act' if dc % 2 else 'dve')
            for fc in range(4):
                for kc in range(8):
                    MM(P, PS[4].t[:, 0:cap], w1e.t[:, kc, fc * 128:(fc + 1) * 128], xeT.t[:, kc, 0:cap], kc == 0, kc == 7, [w1e, xeT], [PS[4]])
                for kc in range(8):
                    MM(P, PS[5].t[:, 0:cap], w3e.t[:, kc, fc * 128:(fc + 1) * 128], xeT.t[:, kc, 0:cap], kc == 0, kc == 7, [w3e, xeT], [PS[5]])
                s1 = s1r.next()
                ACT(P, s1.t[:, 0:cap], PS[4].t[:, 0:cap], AF.Silu, [PS[4]], [s1])
                TT(P, hid.t[:, fc, 0:cap], s1.t[:, 0:cap], PS[5].t[:, 0:cap], ALU.mult, [s1, PS[5]], [hid])
            for sc in range(nsc):
                for db in range(2):
                    pr = PS[2 + db]
                    for fc in range(4):
                        MM(P, pr.t[0:scw, :], hid.t[:, fc, sc * 128:sc * 128 + scw], w2e.t[:, fc, db * 512:(db + 1) * 512], fc == 0, fc == 3, [hid, w2e], [pr])
                    CP(P, ye.t[0:scw, sc, db * 512:(db + 1) * 512], pr.t[0:scw, :], [pr], [ye], eng='act')
            for sc in range(nsc):
                for (b0, m) in tokblocks(0, n):
                    MM(P, PS[6].t[:, 0:m], c['esel'].t[:, e, :], k.posmT.t[:, t0 + b0:t0 + b0 + m], True, True, [c['esel'], k.posmT], [PS[6]])
                    TS(P, selT.t[0:scw, sc, b0:b0 + m], PS[6].t[0:scw, 0:m], c['slotid'].t[0:scw, sc:sc + 1], ALU.is_equal, [PS[6], c['slotid']], [selT])
            for j in range(nt):
                for db in range(2):
                    pr = scat[sci % 3]
                    sci += 1
                    for sc in range(nsc):
                        MM(P, pr.t[:], selT.t[0:scw, sc, j * 128:(j + 1) * 128], ye.t[0:scw, sc, db * 512:(db + 1) * 512], sc == 0, sc == nsc - 1, [selT, ye], [pr])
                    oa = oacc.t[:, j, db * 512:(db + 1) * 512]
                    if e == 0:
                        TS(P, oa, pr.t[:], k.gatek.t[:, j0 + j, e:e + 1], ALU.mult, [pr, k.gatek], [oacc])
                    else:
                        STT(P, oa, pr.t[:], k.gatek.t[:, j0 + j, e:e + 1], oa, ALU.mult, ALU.add, [pr, k.gatek, oacc], [oacc])
        P.dma('pool', 'sto%d' % (si % 4), [(S['yffn'][t0:t0 + n, :].rearrange("(j p) d -> p j d", p=128), oacc.t[:, 0:nt, :])], [oacc], [hS['yffn']])
        si += 1
    P.end()
    P.begin()
    rows = {}
    for which, r in (('lat', s), ('ctx', 2)):
        t = P.sbuf([128, D], F32, 'row')
        load_row(k, t, S['modrow'][l, r, 5 * D:6 * D], 'ld%d' % (len(rows) % 4), [hS['modrow']])
        rows[which] = t
    g2 = P.sbuf([128, D], F32, 'g2')
    b2 = P.sbuf([128, D], F32, 'b2')
    load_row(k, g2, I['ln2_g'][l], 'ld2')
    load_row(k, b2, I['ln2_b'][l], 'ld3')
    xtr = P.rot(2, [128, D], F32, 'xt')
    yr = P.rot(2, [128, D], F32, 'yt')
    outr = P.rot(2, [128, D], F32, 'ot')
    st6r = P.rot(2, [128, 12], F32, 's6')
    mvr = P.rot(2, [128, 4], F32, 'mv')
    tiles = NTILES if l == 0 else 16
    for i in range(tiles):
        which = 'lat' if i < 16 else 'ctx'
        xt, yt = xtr.next(), yr.next()
        P.dma('sp', 'la%d' % (i % 2), [(xt.t[:], S['xm'][i * 128:(i + 1) * 128, :])], [hS['xm']], [xt])
        P.dma('sp', 'lb%d' % (i % 2), [(yt.t[:], S['yffn'][i * 128:(i + 1) * 128, :])], [hS['yffn']], [yt])
        TT(P, yt.t[:], yt.t[:], rows[which].t[:], ALU.mult, [yt, rows[which]], [yt])
        STT(P, yt.t[:], xt.t[:], ALPHA, yt.t[:], ALU.mult, ALU.add, [xt, yt], [yt])
        ot = outr.next()
        layer_norm_tile(P, yt, None, st6r, mvr, g2, b2, ot.t[:], [ot])
        if l == 0:
            P.dma('pool', 'sto%d' % (i % 4), [(S['xc'][s, i * 128:(i + 1) * 128, :], ot.t[:])], [ot], [hS['xc']])
        else:
            P.dma('pool', 'sto%d' % (i % 4), [(k.OUT[s, i * 128:(i + 1) * 128, :], ot.t[:])], [ot], [k.hOUT])
    P.end()


_CACHE = {}


def prep_inputs(inputs):
    f = lambda a: np.ascontiguousarray(np.asarray(a, dtype=np.float32))
    shared = {}
    for nme in INPUT_SHAPES:
        if nme in ('x', 'ctx', 'c3'):
            continue
        shared[nme] = f(inputs[nme]).reshape(INPUT_SHAPES[nme])
    consts = make_consts()
    for nme, v in consts.items():
        shared['c_' + nme] = f(v)
    maps = []
    x = f(inputs['x'])
    ctx = f(inputs['ctx'])
    c = f(inputs['c'])
    cc = f(inputs['c_ctx'])
    for i in range(8):
        m = dict(shared)
        m['x'] = x[2 * i:2 * i + 2]
        m['ctx'] = ctx[2 * i:2 * i + 2]
        m['c3'] = np.ascontiguousarray(np.concatenate([c[2 * i:2 * i + 2], cc[None, :], np.zeros((1, D), np.float32)], axis=0))
        maps.append(m)
    return maps


def kernel(**inputs):
    if 'nc' not in _CACHE:
        _CACHE['nc'] = build()[0]
    nc = _CACHE['nc']
    maps = prep_inputs(inputs)
    res = run_bass_kernel_spmd(nc, maps, core_ids=list(range(8)))
    out = np.concatenate([np.asarray(r['out']) for r in res.results], axis=0)
    return out.astype(np.float32)
```
